# Optimizing a Trainium2 kernel written in Bass

```python
import jax, jax.numpy as jnp
from jax import lax
import numpy as np

D_MODEL = 1024
BATCH = 4
SEQ = 8192
DEPTH = 2
DEC_BATCH = 32
DEC_SEQ = 64
PAST_LEN = 1024

CHUNK = 64
Q_BLOCK = 128
EPS = 1e-6
N_MIXERS = 2
N_GLA_LAYERS = (DEPTH + 1) // 2
N_MLA_LAYERS = DEPTH // 2

D_FF = 2816

GLA_HEADS = 4
GLA_DK = D_MODEL // 2 // GLA_HEADS
GLA_DV = D_MODEL // GLA_HEADS
GLA_GATE_RANK = 16
GLA_GATE_NORM = 16.0
GLA_QK = GLA_HEADS * GLA_DK
GLA_VW = GLA_HEADS * GLA_DV
GLA_IN = 2 * GLA_QK + 2 * GLA_VW + GLA_GATE_RANK

MLA_HEADS = 8
MLA_NOPE = 128
MLA_ROPE = 64
MLA_V = 128
MLA_Q_LORA = 384
MLA_KV_LORA = 256
MLA_DOWN = MLA_Q_LORA + MLA_KV_LORA + MLA_ROPE
MLA_SCALE = (MLA_NOPE + MLA_ROPE) ** -0.5
ROPE_THETA = 10000.0

kernel_name = 'hybrid_gla_mla_macaron_stream_step'


def rmsnorm(x, g):
    xf = x.astype(jnp.float32)
    y = xf * lax.rsqrt(jnp.mean(xf * xf, axis=-1, keepdims=True) + EPS)
    return (y * g.astype(jnp.float32)).astype(x.dtype)


def swiglu(x, w_gate, w_up, w_down):
    return (jax.nn.silu(x @ w_gate) * (x @ w_up)) @ w_down


def rope(x, pos):
    half = MLA_ROPE // 2
    inv = ROPE_THETA ** (-jnp.arange(half, dtype=jnp.float32) / half)
    ang = pos.astype(jnp.float32)[:, None] * inv[None, :]
    ang = ang.reshape((ang.shape[0],) + (1,) * (x.ndim - 3) + (half,))
    cos, sin = jnp.cos(ang), jnp.sin(ang)
    x1, x2 = jnp.split(x.astype(jnp.float32), 2, axis=-1)
    return jnp.concatenate([x1 * cos - x2 * sin, x1 * sin + x2 * cos], axis=-1).astype(x.dtype)


def gla_recurrence(q, k, v, log_a, s0):
    B, L, H, _ = q.shape
    C = min(CHUNK, L)
    nc = L // C

    def to_chunks(t):
        return t.astype(jnp.float32).reshape(B, nc, C, H, t.shape[-1]).transpose(1, 0, 3, 2, 4)

    qc, kc, vc, ac = to_chunks(q), to_chunks(k), to_chunks(v), to_chunks(log_a)
    causal = jnp.tril(jnp.ones((C, C), dtype=bool))

    def step(s, inp):
        qi, ki, vi, ai = inp
        b = jnp.cumsum(ai, axis=-2)
        q_dec = qi * jnp.exp(b)
        k_inv = ki * jnp.exp(-b)
        scores = jnp.where(causal, jnp.einsum('bhtk,bhsk->bhts', q_dec, k_inv), 0.0)
        o = jnp.einsum('bhts,bhsv->bhtv', scores, vi) + jnp.einsum('bhtk,bhkv->bhtv', q_dec, s)
        b_last = b[:, :, -1:, :]
        k_tail = ki * jnp.exp(b_last - b)
        s_new = s * jnp.exp(b_last[:, :, 0, :, None]) + jnp.einsum('bhsk,bhsv->bhkv', k_tail, vi)
        return s_new, o

    s_fin, o = lax.scan(step, s0.astype(jnp.float32), (qc, kc, vc, ac))
    o = o.transpose(1, 0, 3, 2, 4).reshape(B, L, H, -1)
    return o, s_fin


def gla_mixer(h, s0, w_in, w_gk_up, b_gk, g_norm, w_out):
    B, L, _ = h.shape
    proj = h @ w_in
    q, k, v, g, gk_low = jnp.split(proj, [GLA_QK, 2 * GLA_QK, 2 * GLA_QK + GLA_VW, 2 * GLA_QK + 2 * GLA_VW], axis=-1)
    log_a = jax.nn.log_sigmoid((gk_low @ w_gk_up + b_gk).astype(jnp.float32)) / GLA_GATE_NORM
    q = q.reshape(B, L, GLA_HEADS, GLA_DK) * (GLA_DK ** -0.5)
    k = k.reshape(B, L, GLA_HEADS, GLA_DK)
    v = v.reshape(B, L, GLA_HEADS, GLA_DV)
    log_a = log_a.reshape(B, L, GLA_HEADS, GLA_DK)
    o, s_fin = gla_recurrence(q, k, v, log_a, s0)
    o = rmsnorm(o.astype(h.dtype), g_norm).reshape(B, L, GLA_VW) * jax.nn.silu(g)
    return o @ w_out, s_fin.astype(s0.dtype)


def mla_attend(q_lat, q_pe, ckv, kpe, q_pos, k_pos):
    s = jnp.einsum('bqhc,bkc->bhqk', q_lat, ckv) + jnp.einsum('bqhr,bkr->bhqk', q_pe, kpe)
    s = s.astype(jnp.float32) * MLA_SCALE
    visible = (k_pos[None, :] // CHUNK) <= (q_pos[:, None] // CHUNK)
    s = jnp.where(visible, s, -jnp.inf)
    p = jax.nn.softmax(s, axis=-1).astype(ckv.dtype)
    return jnp.einsum('bhqk,bkc->bqhc', p, ckv)


def mla_mixer(h, pos, past_ckv, past_kpe, past_pos, w_down, q_norm, w_uq, kv_norm, w_uk, w_uv, w_out):
    B, L, _ = h.shape
    cq, ckv_raw, kpe_raw = jnp.split(h @ w_down, [MLA_Q_LORA, MLA_Q_LORA + MLA_KV_LORA], axis=-1)
    q = (rmsnorm(cq, q_norm) @ w_uq).reshape(B, L, MLA_HEADS, MLA_NOPE + MLA_ROPE)
    q_nope, q_pe = jnp.split(q, [MLA_NOPE], axis=-1)
    q_pe = rope(q_pe, pos)
    ckv_new = rmsnorm(ckv_raw, kv_norm)
    kpe_new = rope(kpe_raw, pos)
    q_lat = jnp.einsum('bqhd,chd->bqhc', q_nope, w_uk)
    if past_ckv is None:
        ckv, kpe, k_pos = ckv_new, kpe_new, pos
    else:
        ckv = jnp.concatenate([past_ckv, ckv_new], axis=1)
        kpe = jnp.concatenate([past_kpe, kpe_new], axis=1)
        k_pos = jnp.concatenate([past_pos, pos])
    if L > Q_BLOCK:
        nb = L // Q_BLOCK
        qb = q_lat.reshape(B, nb, Q_BLOCK, MLA_HEADS, MLA_KV_LORA).swapaxes(0, 1)
        pb = q_pe.reshape(B, nb, Q_BLOCK, MLA_HEADS, MLA_ROPE).swapaxes(0, 1)
        posb = pos.reshape(nb, Q_BLOCK)
        o_lat = lax.map(lambda a: mla_attend(a[0], a[1], ckv, kpe, a[2], k_pos), (qb, pb, posb))
        o_lat = o_lat.swapaxes(0, 1).reshape(B, L, MLA_HEADS, MLA_KV_LORA)
    else:
        o_lat = mla_attend(q_lat, q_pe, ckv, kpe, pos, k_pos)
    o = jnp.einsum('bqhc,chv->bqhv', o_lat, w_uv).reshape(B, L, MLA_HEADS * MLA_V)
    return o @ w_out, ckv_new, kpe_new


def setup_inputs(seed: int = 0) -> dict:
    key = jax.random.key(seed)
    ks = iter(jax.random.split(key, 32))

    def w(shape, fan_in):
        return jax.random.normal(next(ks), shape, jnp.float32) * (fan_in ** -0.5)

    def gain(shape):
        return 1.0 + 0.02 * jax.random.normal(next(ks), shape, jnp.float32)

    D, F = D_MODEL, D_FF
    return {
        'x_prompt': jax.random.normal(next(ks), (BATCH, SEQ, D), jnp.float32),
        'x_sample': jax.random.normal(next(ks), (DEC_BATCH, DEC_SEQ, D), jnp.float32),
        'state_gla': w((N_GLA_LAYERS, DEC_BATCH, GLA_HEADS, GLA_DK, GLA_DV), GLA_DK),
        'cache_ckv': jax.random.normal(next(ks), (N_MLA_LAYERS, DEC_BATCH, PAST_LEN, MLA_KV_LORA), jnp.float32),
        'cache_kpe': jax.random.normal(next(ks), (N_MLA_LAYERS, DEC_BATCH, PAST_LEN, MLA_ROPE), jnp.float32),
        'norm_ffn1': gain((DEPTH, D)),
        'w_ffn1_gate': w((DEPTH, D, F), D),
        'w_ffn1_up': w((DEPTH, D, F), D),
        'w_ffn1_down': w((DEPTH, F, D), F),
        'norm_mix': gain((DEPTH, D)),
        'norm_ffn2': gain((DEPTH, D)),
        'w_ffn2_gate': w((DEPTH, D, F), D),
        'w_ffn2_up': w((DEPTH, D, F), D),
        'w_ffn2_down': w((DEPTH, F, D), F),
        'w_gla_in': w((N_GLA_LAYERS, D, GLA_IN), D),
        'w_gla_gk_up': w((N_GLA_LAYERS, GLA_GATE_RANK, GLA_QK), GLA_GATE_RANK),
        'b_gla_gk': 0.1 * jax.random.normal(next(ks), (N_GLA_LAYERS, GLA_QK), jnp.float32),
        'gla_out_norm': gain((N_GLA_LAYERS, GLA_DV)),
        'w_gla_out': w((N_GLA_LAYERS, GLA_VW, D), GLA_VW),
        'w_mla_down': w((N_MLA_LAYERS, D, MLA_DOWN), D),
        'mla_q_norm': gain((N_MLA_LAYERS, MLA_Q_LORA)),
        'w_mla_uq': w((N_MLA_LAYERS, MLA_Q_LORA, MLA_HEADS * (MLA_NOPE + MLA_ROPE)), MLA_Q_LORA),
        'mla_kv_norm': gain((N_MLA_LAYERS, MLA_KV_LORA)),
        'w_mla_uk': w((N_MLA_LAYERS, MLA_KV_LORA, MLA_HEADS, MLA_NOPE), MLA_KV_LORA),
        'w_mla_uv': w((N_MLA_LAYERS, MLA_KV_LORA, MLA_HEADS, MLA_V), MLA_KV_LORA),
        'w_mla_out': w((N_MLA_LAYERS, MLA_HEADS * MLA_V, D), MLA_HEADS * MLA_V),
        'norm_final': gain((D,)),
    }


def reference(x_prompt, x_sample, state_gla, cache_ckv, cache_kpe,
              norm_ffn1, w_ffn1_gate, w_ffn1_up, w_ffn1_down, norm_mix,
              norm_ffn2, w_ffn2_gate, w_ffn2_up, w_ffn2_down,
              w_gla_in, w_gla_gk_up, b_gla_gk, gla_out_norm, w_gla_out,
              w_mla_down, mla_q_norm, w_mla_uq, mla_kv_norm, w_mla_uk, w_mla_uv, w_mla_out,
              norm_final):

    def run(x, pos, gla_s0, past_ckv, past_kpe, past_pos):
        new_gla, new_ckv, new_kpe = [], [], []
        for i in range(DEPTH):
            x = x + 0.5 * swiglu(rmsnorm(x, norm_ffn1[i]), w_ffn1_gate[i], w_ffn1_up[i], w_ffn1_down[i])
            h = rmsnorm(x, norm_mix[i])
            j = i // N_MIXERS
            if i % N_MIXERS == 0:
                y, s = gla_mixer(h, gla_s0[j], w_gla_in[j], w_gla_gk_up[j], b_gla_gk[j],
                                 gla_out_norm[j], w_gla_out[j])
                new_gla.append(s)
            else:
                pc = None if past_ckv is None else past_ckv[j]
                pk = None if past_kpe is None else past_kpe[j]
                y, c, r = mla_mixer(h, pos, pc, pk, past_pos, w_mla_down[j], mla_q_norm[j], w_mla_uq[j],
                                    mla_kv_norm[j], w_mla_uk[j], w_mla_uv[j], w_mla_out[j])
                new_ckv.append(c)
                new_kpe.append(r)
            x = x + y
            x = x + 0.5 * swiglu(rmsnorm(x, norm_ffn2[i]), w_ffn2_gate[i], w_ffn2_up[i], w_ffn2_down[i])
        return rmsnorm(x, norm_final), jnp.stack(new_gla), jnp.stack(new_ckv), jnp.stack(new_kpe)

    bp, lp = x_prompt.shape[0], x_prompt.shape[1]
    pos_p = jnp.arange(lp)
    s0_p = jnp.zeros((N_GLA_LAYERS, bp, GLA_HEADS, GLA_DK, GLA_DV), x_prompt.dtype)
    y_prompt, gla_p, ckv_p, kpe_p = run(x_prompt, pos_p, s0_p, None, None, None)

    past_len = cache_ckv.shape[2]
    pos_s = past_len + jnp.arange(x_sample.shape[1])
    past_pos = jnp.arange(past_len)
    y_sample, gla_s, ckv_s, kpe_s = run(x_sample, pos_s, state_gla, cache_ckv, cache_kpe, past_pos)

    return (y_prompt, y_sample, gla_p, ckv_p, kpe_p, gla_s, ckv_s, kpe_s)
```

```python
import numpy as np
import ml_dtypes
import concourse.bass as bass
import concourse.mybir as mybir
from concourse.bass_utils import run_bass_kernel_spmd

F32 = mybir.dt.float32
BF16 = mybir.dt.bfloat16
AF = mybir.ActivationFunctionType
ALU = mybir.AluOpType
AX = mybir.AxisListType

D = 1024
DFF = 2816
NF = DFF // 128
EPS = 1e-6
SEG = 512
NSAMP = 4
SL = 64
PAST = 1024
MLA_SCALE = float((128 + 64) ** -0.5)
NEG = -30000.0
NLANE = 8
SEMCH = 8000


class Op:
    __slots__ = ("eng", "fn", "deps", "dma", "sig", "sidx", "lane", "lval", "cc")


class Prog:
    def __init__(self):
        self.q = {e: [] for e in ("pe", "act", "dve", "pool", "sp")}
        self.lastw = {}
        self.readers = {}
        self.nbank = 0
        self.epoch = None

    def bank(self):
        b = self.nbank % 8
        self.nbank += 1
        return b

    def add(self, eng, fn, r=(), w=(), dma=False, cc=None):
        op = Op()
        op.eng, op.fn, op.dma, op.sig, op.cc = eng, fn, dma, False, cc
        deps = {}
        barrier = "EPOCH" in w
        w = tuple(k for k in w if k != "EPOCH")
        if self.epoch is not None:
            deps[id(self.epoch)] = self.epoch
        if barrier:
            for e, ops in self.q.items():
                nd = 0
                seen_c = False
                for o in reversed(ops):
                    if o.dma:
                        if o.cc is not None or nd < NLANE:
                            deps[id(o)] = o
                            if o.cc is None:
                                nd += 1
                    elif not seen_c:
                        deps[id(o)] = o
                        seen_c = True
                    if seen_c and nd >= NLANE:
                        break
        for k in r:
            lw = self.lastw.get(k)
            if lw is not None:
                deps[id(lw)] = lw
        for k in w:
            lw = self.lastw.get(k)
            if lw is not None:
                deps[id(lw)] = lw
            for ek, rd in self.readers.get(k, {}).items():
                if ek == "dma":
                    for o in rd:
                        deps[id(o)] = o
                else:
                    deps[id(rd)] = rd
        for k in r:
            rdd = self.readers.setdefault(k, {})
            if dma:
                rdd.setdefault("dma", []).append(op)
            else:
                rdd[eng] = op
        for k in w:
            self.lastw[k] = op
            self.readers[k] = {}
        dl = []
        for d in deps.values():
            if d is op:
                continue
            if (not d.dma) and (not dma) and d.eng == "pe" and eng == "pe":
                continue
            if not d.dma:
                d.sig = True
            dl.append(d)
        op.deps = dl
        self.q[eng].append(op)
        if barrier:
            self.epoch = op
        return op

    def finalize(self):
        for e, ops in self.q.items():
            n = 0
            k = 0
            for op in ops:
                if op.dma:
                    if op.cc is None:
                        op.lane = (e, k % NLANE)
                        op.lval = 16 * (k // NLANE + 1)
                        k += 1
                else:
                    if op.sig:
                        op.sidx = n
                        n += 1

    def emit(self, e, eng, sems, lanes):
        waited = {}

        def need(sem, val):
            key = id(sem)
            if waited.get(key, 0) < val:
                eng.wait_ge(sem, val)
                waited[key] = val

        for op in self.q[e]:
            for d in op.deps:
                if d.dma:
                    if d.cc is not None:
                        need(d.cc, 1)
                    else:
                        need(lanes[d.lane], d.lval)
                else:
                    need(sems[d.eng][d.sidx // SEMCH], d.sidx % SEMCH + 1)
            if op.dma:
                if op.cc is not None:
                    op.fn(eng).then_inc(op.cc)
                else:
                    if op.lval > 16:
                        need(lanes[op.lane], op.lval - 16)
                    op.fn(eng).then_inc(lanes[op.lane], 16)
            else:
                inst = op.fn(eng)
                if op.sig:
                    inst.then_inc(sems[e][op.sidx // SEMCH], 1)


def build(NSEG, debug_phase=9):
    NTP = NSEG * SEG
    NTOK = NTP + NSAMP * SL
    nc = bass.Bass("TRN2", target_bir_lowering=False)
    P = Prog()

    def din(name, shape, dt=F32):
        return nc.dram_tensor(name, list(shape), dt, kind="ExternalInput").ap()

    def dout(name, shape, dt=F32):
        return nc.dram_tensor(name, list(shape), dt, kind="ExternalOutput").ap()

    def dscr(name, shape, dt):
        return nc.dram_tensor(name, list(shape), dt)

    x_in = din("x_in", [NTOK, D])
    st_in = din("st_in", [NSAMP, 4, 128, 256])
    cckv = din("cckv", [NSAMP, PAST, 256])
    ckpe = din("ckpe", [NSAMP, PAST, 64])
    wfg = [[din(f"wg{l}{i}", [D, DFF]) for i in range(2)] for l in range(2)]
    wfu = [[din(f"wu{l}{i}", [D, DFF]) for i in range(2)] for l in range(2)]
    wfd = [[din(f"wd{l}{i}", [DFF, D]) for i in range(2)] for l in range(2)]
    w_gin = din("w_gin", [D, 3088])
    w_gout = din("w_gout", [D, D])
    gkw = din("gkw", [33, 512])
    w_dn = din("w_dn", [D, 768])
    w_uq = din("w_uq", [384, 2048])
    w_ukT = din("w_ukT", [D, 256])
    w_uv = din("w_uv", [256, D])
    w_mo = din("w_mo", [D, D])
    gains = din("gains", [128, 56])
    qn_g = din("qn_g", [128, 3])
    kvn_g = din("kvn_g", [128, 2])
    gon = din("gon", [1, 256])
    ropeC = din("ropeC", [64, NTOK])
    ropeS = din("ropeS", [64, NTOK])
    amask = din("amask", [2, 4, 128, 512])
    rsel = din("rsel", [128, 2])
    ident_in = din("ident", [128, 128])
    tri_in = din("tri", [64, 128])
    umask_in = din("umask", [64, 64])

    y_out = dout("y_out", [NTOK, D])
    ckv_out = dout("ckv_out", [NTOK, 256])
    kpe_out = dout("kpe_out", [NTOK, 64])
    gst_p = dout("gst_p", [128, 1024])
    gst_s = dout("gst_s", [NSAMP, 128, 1024])

    bwg = [[dscr(f"bwg{l}{i}", [D, DFF], BF16) for i in range(2)] for l in range(2)]
    bwu = [[dscr(f"bwu{l}{i}", [D, DFF], BF16) for i in range(2)] for l in range(2)]
    bwd = [[dscr(f"bwd{l}{i}", [DFF, D], BF16) for i in range(2)] for l in range(2)]
    b_gin = dscr("b_gin", [D, 3088], BF16)
    b_gout = dscr("b_gout", [D, D], BF16)
    b_dn = dscr("b_dn", [D, 768], BF16)
    b_uq = dscr("b_uq", [384, 2048], BF16)
    b_ukT = dscr("b_ukT", [D, 256], BF16)
    b_uv = dscr("b_uv", [256, D], BF16)
    b_mo = dscr("b_mo", [D, D], BF16)
    XA = dscr("XA", [128, 8, NTOK], F32)
    XB = dscr("XB", [128, 8, NTOK], F32)
    gx_in = [dscr(f"gx_in{j}", [128, 1032], F32) for j in range(NSEG)]
    gx_out = [dscr(f"gx_out{j}", [256, 1032], F32) for j in range(NSEG)]
    smine = dscr("smine", [NSEG, 128, 1024], F32)
    NBLK = NSEG * (SEG // 256)
    gsk = dscr("gsk", [NBLK, 64, 4 * 512], BF16)
    gsv = dscr("gsv", [NBLK, 64, 4 * 1024], BF16)
    gsh = dscr("gsh", [NBLK, 64, 4 * 512], BF16)
    gsl = dscr("gsl", [NBLK, 64, 4 * 512], BF16)
    cqn_s = dscr("cqn_s", [128, 3, NTOK], BF16)
    kxk_in = [dscr(f"kxk_in{j}", [384, SEG], BF16) for j in range(NSEG)]
    kxk_out = [dscr(f"kxk_out{j}", [768, SEG], BF16) for j in range(NSEG)]
    kxv_in = [dscr(f"kxv_in{j}", [SEG, 256], BF16) for j in range(NSEG)]
    kxv_out = [dscr(f"kxv_out{j}", [2 * SEG, 256], BF16) for j in range(NSEG)]
    ksk = dscr("ksk", [384, NSAMP * SL], BF16)
    ksv = dscr("ksv", [NSAMP * SL, 256], BF16)

    SB_BASE = 16384 + 512
    cur = [SB_BASE]
    hi = [0]

    def sb(name, shape, dt):
        esz = 4 if dt == F32 else 2
        n = 1
        for s in shape[1:]:
            n *= s
        nbytes = (n * esz + 31) // 32 * 32
        t = nc.alloc_sbuf_tensor_at(name, list(shape), dt, offset=cur[0])
        cur[0] += nbytes
        hi[0] = max(hi[0], cur[0])
        return t

    ident = sb("ident", [128, 128], F32)
    identb = sb("identb", [128, 128], BF16)
    onesb = sb("onesb", [128, 128], BF16)
    tri = sb("tri", [64, 128], F32)
    trib = sb("trib", [64, 128], BF16)
    umask = sb("umaskt", [64, 64], F32)
    gain_t = sb("gain_t", [128, 56], F32)
    qng = sb("qng", [128, 3], F32)
    kvng = sb("kvng", [128, 2], F32)
    rselt = sb("rselt", [128, 2], F32)
    epst = sb("epst", [128, 1], F32)
    gonb = sb("gonb", [64, 1024], F32)
    gkwf = sb("gkwf", [33, 512], F32)
    gkwb = sb("gkwb", [33, 512], BF16)
    xT = [sb(f"xT{i}", [128, 8, SEG], F32) for i in range(2)]
    hT = sb("hT", [128, 8, SEG], BF16)
    abuf = sb("abuf", [128, NF, SEG], BF16)
    RS = 6
    ring = sb("ring", [128, RS, 2048], BF16)
    stage_off = cur[0]
    stage = sb("stage", [128, 4, D], F32)
    rstd = sb("rstd", [128, SEG], F32)
    sgt = [sb(f"sgt{i}", [128, SEG], F32) for i in range(2)]
    scr_off = cur[0]
    scr = sb("scr", [128, 6, SEG], F32)
    st64 = sb("st64", [128, 4, 64], F32)
    stv = sb("stv", [128, 4, 256], F32)
    stvb = sb("stvb", [128, 4, 256], BF16)
    kbf = sb("kbf", [128, 3, SEG], BF16)
    scr_end = cur[0]
    ropeCt = sb("ropeCt", [64, SEG], F32)
    ropeSt = sb("ropeSt", [64, SEG], F32)
    bar_t = sb("bar_t", [128, 8], F32)
    groups = [[0, 1], [2, 3], [4, 5], [6, 7]]

    def allgather(src, dst, rkey, wkey):
        sem = nc.alloc_semaphore(f"cc_{wkey[0]}{wkey[1]}")
        P.add("pool", lambda e, s_=src, d_=dst: e.collective_compute("AllGather", ALU.bypass, replica_groups=groups,
                                                                      ins=[s_.ap().opt()], outs=[d_.ap().opt()]),
              [rkey], [wkey], dma=True, cc=sem)

    def barrier():
        P.add("pool", lambda e: e.memset(bar_t[:, :], 0.0), [], ["EPOCH"])
    arena0 = cur[0]
    GT = 256
    NCH = GT // 64
    k_tm = sb("k_tm", [64, NCH, 512], BF16)
    v_tm = sb("v_tm", [64, NCH, 1024], BF16)
    gg_tm = sb("gg_tm", [64, NCH, 1024], BF16)
    la_hi = sb("la_hi", [64, NCH, 512], BF16)
    la_lo = sb("la_lo", [64, NCH, 512], BF16)
    laf = sb("laf", [64, 512], F32)
    qTt = sb("qTt", [128, 4, GT], BF16)
    kTt = sb("kTt", [128, 4, GT], BF16)
    gkT = sb("gkT", [33, GT], BF16)
    e1t = sb("e1t", [64, 512], F32)
    ebt2 = [sb(f"ebt{i}", [128, 4, 64], F32) for i in range(2)]
    enbt2 = [sb(f"enbt{i}", [128, 4, 64], F32) for i in range(2)]
    etl2 = [sb(f"etl{i}", [64, 512], F32) for i in range(2)]
    qdec2 = [sb(f"qdec{i}", [128, 4, 64], BF16) for i in range(2)]
    kinv2 = [sb(f"kinv{i}", [128, 4, 64], BF16) for i in range(2)]
    ktail2 = [sb(f"ktail{i}", [64, 512], BF16) for i in range(2)]
    sTm2 = [sb(f"sTm{i}", [64, 4, 64], BF16) for i in range(2)]
    Sst = sb("Sst", [128, 4, 256], F32)
    Sbf = sb("Sbf", [128, 4, 256], BF16)
    Dacc = sb("Dacc", [128, 8], F32)
    ssq2 = [sb(f"ssq{i}", [64, 4], F32) for i in range(2)]
    ogt2 = [sb(f"ogt{i}", [64, 1024], BF16) for i in range(2)]
    ogT = sb("ogT", [128, 8, GT], BF16)
    gla_end = cur[0]
    cur[0] = stage_off
    gxb = [sb("gxb0", [128, 1032], F32)] * 2
    Rch = sb("Rch", [128, 1024], F32)
    Smt = sb("Smt", [128, 1024], F32)
    assert cur[0] <= stage_off + 16384
    cur[0] = arena0
    cqn = sb("cqn", [128, 3, SEG], BF16)
    qnope = [sb(f"qnope{i}", [128, SEG], BF16) for i in range(2)]
    qlat = sb("qlat", [128, 16, SEG], BF16)
    qpe = sb("qpe", [128, 8, SEG], BF16)
    nmr = sb("nmr", [128, 8, 2, 2], F32)
    smt = [sb(f"smt{i}", [128, 8], F32) for i in range(3)]
    kts_off = cur[0]
    KTs = [sb(f"KTs{i}", [128, 3, SEG], BF16) for i in range(3)]
    Vs = [sb(f"Vs{i}", [128, 4, 264], BF16) for i in range(3)]
    kts_end = cur[0]
    Ssb = [sb(f"Ssb{i}", [128, SEG], F32) for i in range(2)]
    Pbf = [sb(f"Pbf{i}", [128, SEG], BF16) for i in range(3)]
    PTs = [sb(f"PTs{i}", [128, 4, 128], BF16) for i in range(2)]
    Obf8 = sb("Obf8", [128, 8, 256], BF16)
    linv8 = sb("linv8", [128, 8], F32)
    _save = cur[0]
    cur[0] = kts_off
    Vp = sb("Vp", [128, 8, 264], BF16)
    kpp = sb("kpp", [128, 8, 64], BF16)
    KTp = sb("KTp", [128, 3, PAST], BF16)
    KTn = sb("KTn", [128, 3, SL], BF16)
    Vn = sb("Vn", [64, 264], BF16)
    assert cur[0] <= kts_end
    cur[0] = _save
    mla_end = cur[0]
    cur[0] = scr_off
    Oacc = [sb(f"Oacc{i}", [128, 8, 264], F32) for i in range(2)]
    mtile = [sb(f"mtile{i}", [128, SEG], F32) for i in range(2)]
    assert cur[0] <= scr_end
    cur[0] = max(mla_end, gla_end)
    assert hi[0] <= 229376 - 256, hi[0]

    import contextlib
    es = contextlib.ExitStack()
    PSg = [es.enter_context(nc.psum_tensor(f"psb{i}", [128, 512], F32)) for i in range(8)]
    PS = PSg
    PSB = [p.bitcast(BF16) for p in PSg]

    def mm(out, lhsT, rhs, start, stop, r, w):
        P.add("pe", lambda e, o=out, l=lhsT, rh=rhs, s=start, t=stop: e.matmul(o, l, rh, start=s, stop=t), r, w)

    def tr(out, in_, idn, r, w):
        P.add("pe", lambda e, o=out, i=in_, d=idn: e.transpose(o, i, d), r, w)

    def act(out, in_, func, r, w, bias=None, scale=None, accum=None):
        kw = {}
        if bias is not None:
            kw["bias"] = bias
        if scale is not None:
            kw["scale"] = scale
        if accum is not None:
            kw["accum_out"] = accum
        P.add("act", lambda e, o=out, i=in_, f=func, k=kw: e.activation(o, i, f, **k), r, w)

    def tt(out, a, b, op, r, w, eng="dve"):
        P.add(eng, lambda e, o=out, x=a, y=b, p=op: e.tensor_tensor(o, x, y, p), r, w)

    def stt(out, in0, scalar, in1, op0, op1, r, w, accum=None):
        if accum is None:
            P.add("dve", lambda e, o=out, a=in0, s=scalar, b=in1, p0=op0, p1=op1: e.scalar_tensor_tensor(o, a, s, b, p0, p1), r, w)
        else:
            P.add("dve", lambda e, o=out, a=in0, s=scalar, b=in1, p0=op0, p1=op1, ac=accum: e.scalar_tensor_tensor(o, a, s, b, p0, p1, accum_out=ac), r, w)

    def ts(out, in0, s1, op0, r, w, s2=None, op1=None, eng="dve"):
        if op1 is None:
            P.add(eng, lambda e, o=out, a=in0, s=s1, p=op0: e.tensor_scalar(o, a, s, None, p), r, w)
        else:
            P.add(eng, lambda e, o=out, a=in0, s=s1, q=s2, p=op0, p1=op1: e.tensor_scalar(o, a, s, q, p, p1), r, w)

    def cp(out, in_, r, w, eng="dve"):
        if eng == "act":
            P.add("act", lambda e, o=out, i=in_: e.copy(o, i), r, w)
        else:
            P.add(eng, lambda e, o=out, i=in_: e.tensor_copy(o, i), r, w)

    def memset(ap, val, w, eng="dve"):
        P.add(eng, lambda e, a=ap, v=val: e.memset(a, v), (), w)

    def dma(out, in_, r, w, q="sp"):
        P.add(q, lambda e, o=out, i=in_: e.dma_start(out=o, in_=i), r, w, dma=True)

    ring_n = [0]
    STG = [("stg", 0), ("stg", 1)]
    stg_n = [0]
    stage_flat = stage[:, :, :].rearrange("p u d -> p (u d)")

    stg_views = [stage_flat[:, 0:2048], stage_flat[:, 2048:4096],
                 xT[1][:, 0:4, :].rearrange("p k t -> p (k t)"), xT[1][:, 4:8, :].rearrange("p k t -> p (k t)")]
    stg_keys = [[("stg", 0)], [("stg", 1)], [("x", 1, k) for k in range(4)], [("x", 1, k) for k in range(4, 8)]]
    pending_wb = []

    def wload_first(src_f32, dst_bf16, shape, wkey):
        sg = stg_n[0] % 4
        stg_n[0] += 1
        s_ = ring_n[0] % RS
        ring_n[0] += 1
        n = 1
        for d_ in shape[1:]:
            n *= d_
        sv = stg_views[sg][:, 0:n]
        rv = ring[:, s_, 0:n]
        if len(shape) == 3:
            sv3 = sv.rearrange("p (a b) -> p a b", a=shape[1])
            rv3 = rv.rearrange("p (a b) -> p a b", a=shape[1])
        else:
            sv3, rv3 = sv, rv
        dma(sv3, src_f32, [], stg_keys[sg], q="sp")
        cp(rv, sv, stg_keys[sg], [("ring", s_)], eng=("act" if sg % 2 else "dve"))
        pending_wb.append((dst_bf16, rv3, ("ring", s_), wkey))
        while len(pending_wb) > 3:
            d_, r_, rk_, wk_ = pending_wb.pop(0)
            dma(d_, r_, [rk_], [wk_], q="sp")
        return rv3, ("ring", s_)

    def flush_wb():
        while pending_wb:
            d_, r_, rk_, wk_ = pending_wb.pop(0)
            dma(d_, r_, [rk_], [wk_], q="sp")

    def wload(src_ap, shape, rkey):
        s = ring_n[0] % RS
        ring_n[0] += 1
        n = 1
        for d_ in shape[1:]:
            n *= d_
        flat = ring[0:shape[0], s, 0:n]
        if len(shape) == 3:
            view = flat.rearrange("p (a b) -> p a b", a=shape[1])
        else:
            view = flat
        dma(view, src_ap, [rkey], [("ring", s)])
        return view, ("ring", s)

    dma(ident[:, :], ident_in, [], ["ident"])
    cp(identb[:, :], ident[:, :], ["ident"], ["identb"])
    memset(onesb[:, :], 1.0, ["onesb"])
    memset(epst[:, :], EPS, ["epst"])
    dma(tri[:, :], tri_in, [], ["tri"])
    cp(trib[:, :], tri[:, :], ["tri"], ["trib"])
    dma(umask[:, :], umask_in, [], ["umask"])
    dma(gain_t[:, :], gains, [], ["gains"])
    dma(qng[:, :], qn_g, [], ["gains"])
    dma(kvng[:, :], kvn_g, [], ["gains"])
    dma(rselt[:, :], rsel, [], ["gains"])
    for h in range(4):
        dma(gonb[:, h * 256:(h + 1) * 256], gon.broadcast_to([64, 256]), [], ["gonb"])
    dma(gkwf[:, :], gkw, [], ["gkwf"])
    cp(gkwb[:, :], gkwf[:, :], ["gkwf"], ["gkwb"])
    memset(gkT[:, :], 0.0, ["gkT"])
    memset(gkT[32:33, :], 1.0, ["gkT"])
    memset(kbf[64:128, 2, :], 0.0, [("kbf", 2)])

    conv_jobs = {1: [], 2: [], 3: []}
    cur_grp = [1]

    def cdma(d_ap, s_ap, key):
        conv_jobs[cur_grp[0]].append(lambda d_=d_ap, s_=s_ap, k_=key: dma(d_, s_, [], [k_], q="pool"))

    def conv(dst, src, rows, cols, key):
        a = 1
        while cols // a > 2048 or cols % a:
            a += 1
        RB = 512
        for r0 in range(0, rows, RB):
            r1 = min(rows, r0 + RB)
            s_ap = src[r0:r1, :].rearrange("r (a c) -> r a c", a=a)
            d_ap = dst.ap()[r0:r1, :].rearrange("r (a c) -> r a c", a=a)
            cdma(d_ap, s_ap, key)

    def conv_cols(dst, src, key):
        for fg in range(NF // 2):
            cdma(dst.ap()[:, fg * 256:(fg + 1) * 256], src[:, fg * 256:(fg + 1) * 256], (key, fg))

    def conv_rows(dst, src, key):
        for f2 in range(NF // 2):
            cdma(dst.ap()[f2 * 256:(f2 + 1) * 256, :], src[f2 * 256:(f2 + 1) * 256, :], (key, f2))

    def emit_conv(grp, n):
        jobs = conv_jobs[grp]
        for _ in range(min(n, len(jobs))):
            jobs.pop(0)()

    cur_grp[0] = 1
    conv(b_gin, w_gin, D, 3088, "b_gin")
    conv(b_gout, w_gout, D, D, "b_gout")
    cur_grp[0] = 2
    for (l, i) in ((0, 1), (1, 0)):
        conv_cols(bwg[l][i], wfg[l][i], f"bwg{l}{i}")
        conv_cols(bwu[l][i], wfu[l][i], f"bwu{l}{i}")
        conv_rows(bwd[l][i], wfd[l][i], f"bwd{l}{i}")
    conv(b_dn, w_dn, D, 768, "b_dn")
    cur_grp[0] = 3
    conv(b_uq, w_uq, 384, 2048, "b_uq")
    conv(b_ukT, w_ukT, D, 256, "b_ukT")
    conv(b_uv, w_uv, 256, D, "b_uv")
    conv(b_mo, w_mo, D, D, "b_mo")
    conv_cols(bwg[1][1], wfg[1][1], "bwg11")
    conv_cols(bwu[1][1], wfu[1][1], "bwu11")
    conv_rows(bwd[1][1], wfd[1][1], "bwd11")
    emit_conv(1, 10 ** 6)
    n2_per_tile = -(-len(conv_jobs[2]) // max(1, NSEG))
    n3_per_tile = -(-len(conv_jobs[3]) // max(1, NSEG))

    def rmsnorm(src, dst, garr, Dn, T, np_=128):
        n = len(src)
        for i, (s_ap, s_k) in enumerate(src):
            act(abuf[0:np_, i, 0:T], s_ap, AF.Square, [s_k], [("a", i)])
        b = P.bank()
        for i in range(n):
            mm(PS[b][0:np_, 0:T], onesb[0:np_, 0:np_], abuf[0:np_, i, 0:T], i == 0, i == n - 1,
               [("a", i), "onesb"], [("ps", b)])
        act(rstd[0:np_, 0:T], PS[b][0:np_, 0:T], AF.Ln, [("ps", b), "epst"], ["rstd"],
            bias=epst[0:np_, 0:1], scale=1.0 / Dn)
        act(rstd[0:np_, 0:T], rstd[0:np_, 0:T], AF.Exp, ["rstd"], ["rstd"], scale=-0.5)
        for i, ((s_ap, s_k), (d_ap, d_k)) in enumerate(zip(src, dst)):
            stt(d_ap, s_ap, garr[i], rstd[0:np_, 0:T], ALU.mult, ALU.mult, [s_k, "rstd", "gains"], [d_k])

    def xk(b, k):
        return ("x", b, k)

    def norm_x(xb, T, gi, dst_fp32=False):
        src = [(xT[xb][:, k, 0:T], xk(xb, k)) for k in range(8)]
        if dst_fp32:
            dst = src
        else:
            dst = [(hT[:, k, 0:T], ("h", k)) for k in range(8)]
        rmsnorm(src, dst, [gain_t[:, gi * 8 + k:gi * 8 + k + 1] for k in range(8)], D, T)

    def ffn(xb, T, l, i, gi, first=False):
        norm_x(xb, T, gi)
        wkey = f"{l}{i}"
        gv = bwg[l][i].ap().rearrange("(kc p) n -> p kc n", p=128)
        uv = bwu[l][i].ap().rearrange("(kc p) n -> p kc n", p=128)
        dv = bwd[l][i].ap().rearrange("(f p) n -> p f n", p=128)
        for fg in range(NF // 2):
            if first:
                sg_v, sg_k = wload_first(wfg[l][i].rearrange("(kc p) n -> p kc n", p=128)[:, :, fg * 256:(fg + 1) * 256],
                                         gv[:, :, fg * 256:(fg + 1) * 256], [128, 8, 256], ("bwg" + wkey, fg))
                su_v, su_k = wload_first(wfu[l][i].rearrange("(kc p) n -> p kc n", p=128)[:, :, fg * 256:(fg + 1) * 256],
                                         uv[:, :, fg * 256:(fg + 1) * 256], [128, 8, 256], ("bwu" + wkey, fg))
            else:
                sg_v, sg_k = wload(gv[:, :, fg * 256:(fg + 1) * 256], [128, 8, 256], ("bwg" + wkey, fg))
                su_v, su_k = wload(uv[:, :, fg * 256:(fg + 1) * 256], [128, 8, 256], ("bwu" + wkey, fg))
            for fi in range(2):
                f = 2 * fg + fi
                bg = P.bank()
                bu = P.bank()
                for k in range(8):
                    mm(PS[bg][:, 0:T], sg_v[:, k, fi * 128:(fi + 1) * 128], hT[:, k, 0:T], k == 0, k == 7,
                       [sg_k, ("h", k)], [("ps", bg)])
                for k in range(8):
                    mm(PS[bu][:, 0:T], su_v[:, k, fi * 128:(fi + 1) * 128], hT[:, k, 0:T], k == 0, k == 7,
                       [su_k, ("h", k)], [("ps", bu)])
                st = sgt[f % 2]
                act(st[:, 0:T], PS[bg][:, 0:T], AF.Silu, [("ps", bg)], [("sgt", f % 2)])
                tt(abuf[:, f, 0:T], st[:, 0:T], PS[bu][:, 0:T], ALU.mult, [("sgt", f % 2), ("ps", bu)], [("a", f)])
        for f2 in range(NF // 2):
            if first:
                sd_v, sd_k = wload_first(wfd[l][i].rearrange("(f p) n -> p f n", p=128)[:, 2 * f2:2 * f2 + 2, :],
                                         dv[:, 2 * f2:2 * f2 + 2, :], [128, 2, 1024], ("bwd" + wkey, f2))
            else:
                sd_v, sd_k = wload(dv[:, 2 * f2:2 * f2 + 2, :], [128, 2, 1024], ("bwd" + wkey, f2))
            for fi in range(2):
                f = 2 * f2 + fi
                for m in range(8):
                    mm(PS[m][:, 0:T], sd_v[:, fi, m * 128:(m + 1) * 128], abuf[:, f, 0:T], f == 0, f == NF - 1,
                       [sd_k, ("a", f)], [("ps", m)])
        if first:
            flush_wb()
        for m in range(8):
            stt(xT[xb][:, m, 0:T], PS[m][:, 0:T], 0.5, xT[xb][:, m, 0:T], ALU.mult, ALU.add,
                [("ps", m), xk(xb, m)], [xk(xb, m)])

    def load_x_tokmajor(xb, t0, T):
        nu = T // 128
        dma(stage[:, 0:nu, :], x_in[t0:t0 + T, :].rearrange("(u p) d -> p u d", p=128), [], STG)
        for k in range(8):
            b = P.bank()
            for u in range(nu):
                tr(PS[b][:, u * 128:(u + 1) * 128], stage[:, u, k * 128:(k + 1) * 128], ident[:, :],
                   STG + ["ident"], [("ps", b)])
            cp(xT[xb][:, k, 0:T], PS[b][:, 0:T], [("ps", b)], [xk(xb, k)], eng=("act" if k % 2 else "dve"))

    def store_x(xb, dst, t0, T, key):
        dma(dst.ap()[:, :, t0:t0 + T], xT[xb][:, :, 0:T], [xk(xb, k) for k in range(8)], [key], q="pool")

    def load_x(xb, src, t0, T, key):
        dma(xT[xb][:, :, 0:T], src.ap()[:, :, t0:t0 + T], [key], [xk(xb, k) for k in range(8)], q="sp")

    tiles = [(j * SEG, SEG, j) for j in range(NSEG)] + [(NTP, NSAMP * SL, -1)]

    gin_v = b_gin.ap().rearrange("(kc p) n -> p kc n", p=128)
    gout_v = b_gout.ap().rearrange("(f p) n -> p f n", p=128)

    def gla_block(xb, c0, full, seg_j, blk_idx, T):
        cols = slice(c0, c0 + GT)
        hk = [("h", k) for k in range(8)]

        def proj_fm(col0, ncols, dst_fn, scale):
            sv, sk = wload(gin_v[:, :, col0:col0 + ncols], [128, 8, ncols], "b_gin")
            for hi_ in range(ncols // 128):
                b = P.bank()
                for k in range(8):
                    mm(PS[b][:, 0:GT], sv[:, k, hi_ * 128:(hi_ + 1) * 128], hT[:, k, cols], k == 0, k == 7,
                       [sk, ("h", k)], [("ps", b)])
                d_ap, d_k = dst_fn(hi_)
                act(d_ap, PS[b][:, 0:GT], AF.Copy, [("ps", b)], [d_k], scale=scale)

        def proj_tm(col0, dst_fn, silu=False):
            sv, sk = wload(gin_v[:, :, col0:col0 + 256], [128, 8, 256], "b_gin")
            for c in range(NCH):
                b = P.bank()
                for k in range(8):
                    mm(PS[b][0:64, 0:256], hT[:, k, c0 + c * 64:c0 + (c + 1) * 64], sv[:, k, :], k == 0, k == 7,
                       [sk, ("h", k)], [("ps", b)])
                d_ap, d_k, r_extra = dst_fn(c)
                if silu:
                    act(sgt[0][0:64, 0:256], PS[b][0:64, 0:256], AF.Silu, [("ps", b)], [("sgt", 0)])
                    tt(d_ap, sgt[0][0:64, 0:256], r_extra, ALU.mult, [("sgt", 0), "gonb"], [d_k])
                else:
                    cp(d_ap, PS[b][0:64, 0:256], [("ps", b)], [d_k], eng=("act" if c % 2 else "dve"))

        reuse = full and seg_j >= 0
        save = (not full) and seg_j >= 0
        bid = seg_j * (SEG // GT) + blk_idx
        allc = list(range(NCH))
        if reuse:
            dma(k_tm[:, :, :].rearrange("p c n -> p (c n)"), gsk.ap()[bid], [("gs", bid)], [("k_tm", c) for c in allc], q="sp")
            dma(v_tm[:, :, :].rearrange("p c n -> p (c n)"), gsv.ap()[bid], [("gs", bid)], [("v_tm", c) for c in allc], q="sp")
            dma(la_hi[:, :, :].rearrange("p c n -> p (c n)"), gsh.ap()[bid], [("gs", bid)], [("lah", c) for c in allc], q="sp")
            dma(la_lo[:, :, :].rearrange("p c n -> p (c n)"), gsl.ap()[bid], [("gs", bid)], [("lal", c) for c in allc], q="sp")
        if not reuse:
          sv, sk = wload(gin_v[:, :, 3072:3088], [128, 8, 16], "b_gin")
          b = P.bank()
          for k in range(8):
            mm(PS[b][0:16, 0:GT], sv[:, k, :], hT[:, k, cols], k == 0, k == 7, [sk, ("h", k)], [("ps", b)])
          cp(gkT[0:16, :], PS[b][0:16, 0:GT], [("ps", b)], ["gkT"])
          for c in range(NCH):
            b = P.bank()
            mm(PS[b][0:64, :], gkT[0:33, c * 64:(c + 1) * 64], gkwb[0:33, :], True, True, ["gkT", "gkwb"], [("ps", b)])
            act(e1t[:, :], PS[b][0:64, :], AF.Exp, [("ps", b)], ["e1t"], scale=-1.0)
            act(laf[:, :], e1t[:, :], AF.Ln, ["e1t"], ["laf"], bias=1.0)
            cp(la_hi[:, c, :], laf[:, :], ["laf"], [("lah", c)])
            tt(la_lo[:, c, :], laf[:, :], la_hi[:, c, :], ALU.subtract, ["laf", ("lah", c)], [("lal", c)])
        if full:
            for s_ in range(2):
                proj_fm(s_ * 256, 256, lambda hi_, s_=s_: (qTt[:, 2 * s_ + hi_, :], ("qT", 2 * s_ + hi_)), 128 ** -0.5)
            for s_ in range(2):
                proj_fm(512 + s_ * 256, 256, lambda hi_, s_=s_: (kTt[:, 2 * s_ + hi_, :], ("kT", 2 * s_ + hi_)), 1.0)
        if not reuse:
            for s_ in range(2):
                proj_tm(512 + s_ * 256, lambda c, s_=s_: (k_tm[:, c, s_ * 256:(s_ + 1) * 256], ("k_tm", c), None))
            for s_ in range(4):
                proj_tm(1024 + s_ * 256, lambda c, s_=s_: (v_tm[:, c, s_ * 256:(s_ + 1) * 256], ("v_tm", c), None))
        if save:
            dma(gsk.ap()[bid], k_tm[:, :, :].rearrange("p c n -> p (c n)"), [("k_tm", c) for c in allc], [("gs", bid)], q="pool")
            dma(gsv.ap()[bid], v_tm[:, :, :].rearrange("p c n -> p (c n)"), [("v_tm", c) for c in allc], [("gs", bid)], q="pool")
            dma(gsh.ap()[bid], la_hi[:, :, :].rearrange("p c n -> p (c n)"), [("lah", c) for c in allc], [("gs", bid)], q="pool")
            dma(gsl.ap()[bid], la_lo[:, :, :].rearrange("p c n -> p (c n)"), [("lal", c) for c in allc], [("gs", bid)], q="pool")
        if full:
            for s_ in range(4):
                proj_tm(2048 + s_ * 256,
                        lambda c, s_=s_: (gg_tm[:, c, s_ * 256:(s_ + 1) * 256], ("gg_tm", c),
                                          gonb[:, s_ * 256:(s_ + 1) * 256]), silu=True)
        def stageA(c):
            p = c % 2
            ebt, enbt, etl, qdec, kinv, ktail, sTm = ebt2[p], enbt2[p], etl2[p], qdec2[p], kinv2[p], ktail2[p], sTm2[p]
            b = P.bank()
            for h in range(4):
                mm(PS[b][:, h * 64:(h + 1) * 64], la_hi[:, c, h * 128:(h + 1) * 128], trib[:, 0:64], True, False,
                   [("lah", c), "trib"], [("ps", b)])
                mm(PS[b][:, h * 64:(h + 1) * 64], la_lo[:, c, h * 128:(h + 1) * 128], trib[:, 0:64], False, True,
                   [("lal", c), "trib"], [("ps", b)])
            b2 = P.bank()
            mm(PS[b2][0:64, :], trib[:, 64:128], la_hi[:, c, :], True, False, [("lah", c), "trib"], [("ps", b2)])
            mm(PS[b2][0:64, :], trib[:, 64:128], la_lo[:, c, :], False, True, [("lal", c), "trib"], [("ps", b2)])
            act(ebt[:, :, :].rearrange("p h t -> p (h t)"), PS[b][:, 0:256], AF.Exp, [("ps", b)], [("eb", p)])
            if full:
                act(enbt[:, :, :].rearrange("p h t -> p (h t)"), PS[b][:, 0:256], AF.Exp, [("ps", b)], [("enb", p)], scale=-1.0)
            act(etl[:, :], PS[b2][0:64, :], AF.Exp, [("ps", b2)], [("etl", p)])
            if full:
                cc_ = slice(c * 64, (c + 1) * 64)
                tt(qdec[:, :, :], qTt[:, :, cc_], ebt[:, :, :], ALU.mult, [("qT", h) for h in range(4)] + [("eb", p)], [("qdec", p)])
                tt(kinv[:, :, :], kTt[:, :, cc_], enbt[:, :, :], ALU.mult, [("kT", h) for h in range(4)] + [("enb", p)], [("kinv", p)])
                b3 = P.bank()
                for h in range(4):
                    mm(PS[b3][0:64, h * 64:(h + 1) * 64], kinv[:, h, :], qdec[:, h, :], True, True,
                       [("kinv", p), ("qdec", p)], [("ps", b3)])
                tt(sTm[:, :, :], PS[b3][0:64, 0:256].rearrange("p (h t) -> p h t", h=4),
                   umask[:, :].rearrange("p (o t) -> p o t", o=1).broadcast_to([64, 4, 64]), ALU.mult,
                   [("ps", b3), "umask"], [("sTm", p)])
            tt(ktail[:, :], k_tm[:, c, :], etl[:, :], ALU.mult, [("k_tm", c), ("etl", p)], [("ktail", p)])
            if seg_j >= 0 and not full:
                tt(Dacc[:, 0:4], Dacc[:, 0:4], ebt[:, :, 63], ALU.mult, ["Dacc", ("eb", p)], ["Dacc"])

        def stageB1(c):
            p = c % 2
            ebt, qdec, ktail, sTm, ssq, ogt = ebt2[p], qdec2[p], ktail2[p], sTm2[p], ssq2[p], ogt2[p]
            if seg_j < 0:
                sq_ = c
                dma(Sst[:, :, :], st_in[sq_].rearrange("h p v -> p h v"), [], [("S", h) for h in range(4)], q="sp")
                for h in range(4):
                    cp(Sbf[:, h, :], Sst[:, h, :], [("S", h)], [("Sbf", h)], eng="act")
            if full:
                bo = [P.bank(), P.bank()]
                for h in range(4):
                    o_ps = PS[bo[h // 2]][0:64, (h % 2) * 256:(h % 2 + 1) * 256]
                    mm(o_ps, sTm[:, h, :], v_tm[:, c, h * 256:(h + 1) * 256], True, False,
                       [("sTm", p), ("v_tm", c)], [("ps", bo[h // 2])])
                    mm(o_ps, qdec[:, h, :], Sbf[:, h, :], False, True, [("qdec", p), ("Sbf", h)], [("ps", bo[h // 2])])
            bk = [P.bank(), P.bank()]
            for h in range(4):
                mm(PS[bk[h // 2]][:, (h % 2) * 256:(h % 2 + 1) * 256], ktail[:, h * 128:(h + 1) * 128],
                   v_tm[:, c, h * 256:(h + 1) * 256], True, True, [("ktail", p), ("v_tm", c)], [("ps", bk[h // 2])])
            for h in range(4):
                stt(Sst[:, h, :], Sst[:, h, :], ebt[:, h, 63:64], PS[bk[h // 2]][:, (h % 2) * 256:(h % 2 + 1) * 256],
                    ALU.mult, ALU.add, [("S", h), ("eb", p), ("ps", bk[h // 2])], [("S", h)])
                if seg_j >= 0:
                    cp(Sbf[:, h, :], Sst[:, h, :], [("S", h)], [("Sbf", h)], eng="act")
            if seg_j < 0:
                dma(gst_s[c], Sst[:, :, :].rearrange("p h v -> p (h v)"), [("S", h) for h in range(4)], [("gst_s", c)], q="pool")
            if full:
                for h in range(4):
                    o_ps = PS[bo[h // 2]][0:64, (h % 2) * 256:(h % 2 + 1) * 256]
                    act(e1t[:, 0:256], o_ps, AF.Square, [("ps", bo[h // 2])], ["e1t", ("ssq", p, h)], accum=ssq[:, h:h + 1])
                act(ssq[:, :], ssq[:, :], AF.Ln, [("ssq", p, h) for h in range(4)] + ["epst"], [("ssq", p, h) for h in range(4)],
                    bias=epst[0:64, 0:1], scale=1.0 / 256)
                act(ssq[:, :], ssq[:, :], AF.Exp, [("ssq", p, h) for h in range(4)], [("ssq", p, h) for h in range(4)], scale=-0.5)
                for h in range(4):
                    o_ps = PS[bo[h // 2]][0:64, (h % 2) * 256:(h % 2 + 1) * 256]
                    stt(ogt[:, h * 256:(h + 1) * 256], o_ps, ssq[:, h:h + 1], gg_tm[:, c, h * 256:(h + 1) * 256],
                        ALU.mult, ALU.mult, [("ps", bo[h // 2]), ("ssq", p, h), ("gg_tm", c)], [("og", p, h)])

        def stageB2(c):
            if not full:
                return
            p = c % 2
            ogt = ogt2[p]
            bt = P.bank()
            for k in range(8):
                tr(PSB[bt][:, k * 64:(k + 1) * 64], ogt[:, k * 128:(k + 1) * 128], identb[0:64, 0:64],
                   [("og", p, k // 2), "identb"], [("ps", bt)])
            cp(ogT[:, :, c * 64:(c + 1) * 64], PSB[bt][:, 0:512].rearrange("p (k t) -> p k t", k=8),
               [("ps", bt)], [("ogT", c)], eng="act")

        stageA(0)
        for c in range(NCH):
            if c + 1 < NCH:
                stageA(c + 1)
            stageB1(c)
            if c >= 1:
                stageB2(c - 1)
        stageB2(NCH - 1)
        if full:
            sls = []
            for f2 in range(4):
                sls.append(wload(gout_v[:, 2 * f2:2 * f2 + 2, :], [128, 2, 1024], "b_gout"))
            for m in range(8):
                b = P.bank()
                for k in range(8):
                    sv2, sk2 = sls[k // 2]
                    mm(PS[b][:, 0:GT], sv2[:, k % 2, m * 128:(m + 1) * 128], ogT[:, k, :], k == 0, k == 7,
                       [sk2] + [("ogT", c) for c in range(NCH)], [("ps", b)])
                tt(xT[xb][:, m, cols], PS[b][:, 0:GT], xT[xb][:, m, cols], ALU.add, [("ps", b), xk(xb, m)], [xk(xb, m)])

    for ti, (t0, T, j) in enumerate(tiles):
        xb = ti % 2
        load_x_tokmajor(xb, t0, T)
        ffn(xb, T, 0, 0, 0, first=(ti == 0))
        store_x(xb, XA, t0, T, ("XA", ti))
        if j >= 0:
            norm_x(xb, T, 1)
            memset(Sst[:, :, :], 0.0, [("S", h) for h in range(4)])
            memset(Dacc[:, :], 1.0, ["Dacc"])
            for bi in range(T // GT):
                gla_block(xb, bi * GT, False, j, bi, T)
            dma(gx_in[j].ap()[:, 0:1024], Sst[:, :, :].rearrange("p h v -> p (h v)"), [("S", h) for h in range(4)], [("gx_in", j)], q="pool")
            dma(gx_in[j].ap()[:, 1024:1032], Dacc[:, :], ["Dacc"], [("gx_in", j)], q="pool")
            allgather(gx_in[j], gx_out[j], ("gx_in", j), ("gx_out", j))
            emit_conv(2, n2_per_tile)

    emit_conv(2, 10 ** 6)
    memset(Rch[:, :], 0.0, ["R"] + STG)
    for g in range(2 * NSEG):
        rr, jj = g % 2, g // 2
        gb = gxb[g % 2]
        dma(gb[:, :], gx_out[jj].ap()[rr * 128:(rr + 1) * 128, :], [("gx_out", jj)] + STG, [("gxb", 0)] + STG, q="sp")
        if rr == 0:
            ts(Smt[:, :], Rch[:, :], rselt[:, 0:1], ALU.mult, ["R", "gains"] + STG, ["Sm"] + STG)
        else:
            stt(Smt[:, :], Rch[:, :], rselt[:, 1:2], Smt[:, :], ALU.mult, ALU.add, ["R", "gains", "Sm"] + STG, ["Sm"] + STG)
            dma(smine.ap()[jj], Smt[:, :], ["Sm"] + STG, [("smine", jj)] + STG, q="pool")
        for h in range(4):
            stt(Rch[:, h * 256:(h + 1) * 256], Rch[:, h * 256:(h + 1) * 256], gb[:, 1024 + h:1025 + h],
                gb[:, h * 256:(h + 1) * 256], ALU.mult, ALU.add, ["R", ("gxb", 0)] + STG, ["R"] + STG)
    dma(gst_p, Rch[:, :], ["R"] + STG, ["gst_p"] + STG, q="pool")

    dn_v = b_dn.ap().rearrange("(kc p) n -> p kc n", p=128)
    for ti, (t0, T, j) in enumerate(tiles):
        xb = ti % 2
        load_x(xb, XA, t0, T, ("XA", ti))
        norm_x(xb, T, 1)
        if j >= 0:
            dma(Sst[:, :, :].rearrange("p h v -> p (h v)"), smine.ap()[j], [("smine", j)], [("S", h) for h in range(4)], q="sp")
            cp(Sbf[:, :, :], Sst[:, :, :], [("S", h) for h in range(4)], [("Sbf", h) for h in range(4)], eng="act")
        for bi in range(T // GT):
            gla_block(xb, bi * GT, True, j, bi, T)
        ffn(xb, T, 0, 1, 2)
        ffn(xb, T, 1, 0, 3)
        store_x(xb, XB, t0, T, ("XB", ti))
        norm_x(xb, T, 4)
        dma(ropeCt[:, 0:T], ropeC[:, t0:t0 + T], [], ["ropeC"], q="sp")
        dma(ropeSt[:, 0:T], ropeS[:, t0:t0 + T], [], ["ropeS"], q="sp")
        slabs = [wload(dn_v[:, :, s_ * 256:(s_ + 1) * 256], [128, 8, 256], "b_dn") for s_ in range(3)]

        def dnproj(col0, M):
            sv, sk = slabs[col0 // 256]
            lc = col0 % 256
            b = P.bank()
            for k in range(8):
                mm(PS[b][0:M, 0:T], sv[:, k, lc:lc + M], hT[:, k, 0:T], k == 0, k == 7, [sk, ("h", k)], [("ps", b)])
            return b

        for i in range(3):
            b = dnproj(i * 128, 128)
            cp(scr[:, i, 0:T], PS[b][:, 0:T], [("ps", b)], [("scr", i)], eng=("act" if i % 2 else "dve"))
        for i in range(2):
            b = dnproj(384 + i * 128, 128)
            cp(scr[:, 3 + i, 0:T], PS[b][:, 0:T], [("ps", b)], [("scr", 3 + i)], eng=("act" if i % 2 else "dve"))
        b1 = dnproj(640, 64)
        b2 = dnproj(704, 64)
        tt(rstd[0:64, 0:T], PS[b1][0:64, 0:T], ropeCt[:, 0:T], ALU.mult, [("ps", b1), "ropeC"], ["rstd"])
        tt(scr[0:64, 5, 0:T], PS[b2][0:64, 0:T], ropeSt[:, 0:T], ALU.mult, [("ps", b2), "ropeS"], [("scr", 5)])
        tt(scr[0:64, 5, 0:T], scr[0:64, 5, 0:T], rstd[0:64, 0:T], ALU.add, [("scr", 5), "rstd"], [("scr", 5)])
        rmsnorm([(scr[:, i, 0:T], ("scr", i)) for i in range(3)], [(hT[:, i, 0:T], ("h", i)) for i in range(3)],
                [qng[:, i:i + 1] for i in range(3)], 384, T)
        dma(cqn_s.ap()[:, :, t0:t0 + T], hT[:, 0:3, 0:T], [("h", i) for i in range(3)], [("cqn_s", ti)], q="pool")
        rmsnorm([(scr[:, 3 + i, 0:T], ("scr", 3 + i)) for i in range(2)], [(scr[:, 3 + i, 0:T], ("scr", 3 + i)) for i in range(2)],
                [kvng[:, i:i + 1] for i in range(2)], 256, T)
        for i in range(2):
            cp(kbf[:, i, 0:T], scr[:, 3 + i, 0:T], [("scr", 3 + i)], [("kbf", i)], eng=("act" if i else "dve"))
        cp(kbf[0:64, 2, 0:T], scr[0:64, 5, 0:T], [("scr", 5)], [("kbf", 2)])
        if j >= 0:
            kdst = kxk_in[j].ap()[:, :]
            kkey = ("kxk_in", j)
        else:
            kdst = ksk.ap()[:, :]
            kkey = "ksk"
        dma(kdst.rearrange("(c p) t -> p c t", p=128), kbf[:, :, 0:T], [("kbf", 0), ("kbf", 1), ("kbf", 2)], [kkey], q="pool")
        nu = T // 128
        for i in range(2):
            b = P.bank()
            for u in range(nu):
                tr(PS[b][:, u * 128:(u + 1) * 128], scr[:, 3 + i, u * 128:(u + 1) * 128], ident[:, :],
                   [("scr", 3 + i), "ident"], [("ps", b)])
            cp(stv[:, 0:nu, i * 128:(i + 1) * 128], PS[b][:, 0:nu * 128].rearrange("p (u c) -> p u c", u=nu),
               [("ps", b)], ["stv"], eng=("act" if i else "dve"))
        b = P.bank()
        for u in range(nu):
            tr(PS[b][:, u * 64:(u + 1) * 64], scr[0:64, 5, u * 128:(u + 1) * 128], ident[0:64, 0:64],
               [("scr", 5), "ident"], [("ps", b)])
        cp(st64[:, 0:nu, :], PS[b][:, 0:nu * 64].rearrange("p (u c) -> p u c", u=nu), [("ps", b)], ["st64"])
        cp(stvb[:, 0:nu, :], stv[:, 0:nu, :], ["stv"], ["stvb"], eng="act")
        dma(ckv_out[t0:t0 + T, :].rearrange("(u p) c -> p u c", p=128), stv[:, 0:nu, :], ["stv"], [("ckv_out", ti)], q="pool")
        dma(kpe_out[t0:t0 + T, :].rearrange("(u p) c -> p u c", p=128), st64[:, 0:nu, :], ["st64"], [("kpe_out", ti)], q="pool")
        if j >= 0:
            dma(kxv_in[j].ap()[:, :].rearrange("(u p) c -> p u c", p=128), stvb[:, 0:nu, :], ["stvb"], [("kxv_in", j)], q="pool")
            allgather(kxk_in[j], kxk_out[j], ("kxk_in", j), ("kxk_out", j))
            allgather(kxv_in[j], kxv_out[j], ("kxv_in", j), ("kxv_out", j))
            emit_conv(3, n3_per_tile)
        else:
            dma(ksv.ap()[:, :].rearrange("(u p) c -> p u c", p=128), stvb[:, 0:nu, :], ["stvb"], ["ksv"], q="pool")

    emit_conv(3, 10 ** 6)
    barrier()

    uq_v = b_uq.ap().rearrange("(kc p) n -> p kc n", p=128)
    ukT_v = b_ukT.ap().rearrange("(h p) c -> p h c", p=128)
    uv_v = b_uv.ap().rearrange("(cc p) n -> p cc n", p=128)
    mo_v = b_mo.ap().rearrange("(f p) n -> p f n", p=128)
    unit_n = [0]
    kv_n = [0]
    memset(qpe[64:128, :, :], 0.0, [("qpe", h) for h in range(8)], eng="pool")
    for i_ in range(3):
        memset(Vs[i_][:, :, 256:264], 1.0, [("Vs", i_)])

    SBK, TBK, OBK = [0, 1, 2], [3, 4], [5, 6]
    NM0 = float(-MLA_SCALE * NEG)

    def att_unit(h, up, step, qn, qc0, KT, KTk, kn, Vlist, maskap, maskkey, pre=None, post=None):
        n = unit_n[0]
        unit_n[0] += 1
        sl, s2 = n % 3, n % 2
        bs, bt, bo = SBK[sl], TBK[s2], OBK[s2]
        st = smt[sl]
        nkb = len(Vlist)
        kpm = max(kp for (_, kp, _) in Vlist)
        if maskap is not None:
            src, srck = Ssb[s2][0:qn, 0:kn], ("Ssb", s2)
        else:
            src, srck = PS[bs][0:qn, 0:kn], ("ps", bs)
        nm_old = nmr[0:qn, h, up, (step + 1) % 2:(step + 1) % 2 + 1]
        nm_new = nmr[0:qn, h, up, step % 2:step % 2 + 1]
        ko, kn_ = ("nm", h, up, (step + 1) % 2), ("nm", h, up, step % 2)

        def qk():
            if pre is not None:
                pre()
            for c in range(2):
                mm(PS[bs][0:qn, 0:kn], qlat[:, 2 * h + c, qc0:qc0 + qn], KT(c), c == 0, False,
                   [("qlat", h), KTk], [("ps", bs)])
            mm(PS[bs][0:qn, 0:kn], qpe[:, h, qc0:qc0 + qn], KT(2), False, True, [("qpe", h), KTk], [("ps", bs)])
            if maskap is not None:
                tt(src, PS[bs][0:qn, 0:kn], maskap, ALU.add, [("ps", bs), maskkey], [srck])

        def sm():
            srcs = src.rearrange("p (a b) -> p a b", b=4)[:, :, 0]
            P.add("dve", lambda e, o=st[0:qn, 0:1], i=srcs: e.tensor_reduce(o, i, AX.X, ALU.max), [srck], [("smt", sl, 0)])
            ts(nm_new, st[0:qn, 0:1], -MLA_SCALE, ALU.mult, [("smt", sl, 0), ko], [kn_], s2=nm_old, op1=ALU.min)
            act(Pbf[sl][0:qn, 0:kn], src, AF.Exp, [srck, kn_], [("P", sl)], bias=nm_new, scale=MLA_SCALE)
            act(st[0:qn, 3:4], nm_old, AF.Exp, [ko, kn_], [("smt", sl, 3)], bias=nm_new, scale=-1.0)

        def tp():
            for kb, (Vap, kp, Vk) in enumerate(Vlist):
                tr(PSB[bt][0:kp, kb * 128:kb * 128 + qn], Pbf[sl][0:qn, kb * 128:kb * 128 + kp], identb[0:qn, 0:qn],
                   [("P", sl), "identb"], [("ps", bt)])
            cp(PTs[s2][0:kpm, 0:nkb, 0:qn], PSB[bt][0:kpm, 0:nkb * 128].rearrange("p (k q) -> p k q", k=nkb)[:, :, 0:qn],
               [("ps", bt)], [("PT", s2)], eng=("act" if n % 2 else "dve"))

        def pv():
            for kb, (Vap, kp, Vk) in enumerate(Vlist):
                mm(PS[bo][0:qn, 0:264], PTs[s2][0:kp, kb, 0:qn], Vap, kb == 0, kb == nkb - 1, [("PT", s2), Vk], [("ps", bo)])
            stt(Oacc[up][0:qn, h, :], Oacc[up][0:qn, h, :], st[0:qn, 3:4], PS[bo][0:qn, 0:264], ALU.mult, ALU.add,
                [("O", up, h), ("smt", sl, 3), ("ps", bo)], [("O", up, h)])
            if post is not None:
                post()

        return (qk, sm, tp, pv)

    def run_units(units):
        N = len(units)
        for i in range(N + 3):
            if i < N:
                units[i][0]()
            if 0 <= i - 1 < N:
                units[i - 1][1]()
            if 0 <= i - 2 < N:
                units[i - 2][2]()
            if 0 <= i - 3 < N:
                units[i - 3][3]()

    def att_init(up):
        memset(nmr[:, :, up, :], NM0, [("nm", h, up, q_) for h in range(8) for q_ in range(2)])
        memset(Oacc[up][:, :, :], 0.0, [("O", up, h) for h in range(8)], eng="pool")

    def att_fin(up, qn, qc0):
        P.add("dve", lambda e, o=linv8[0:qn, :], i=Oacc[up][0:qn, :, 256]: e.reciprocal(o, i), [("O", up, h) for h in range(8)], ["linv8"])
        for h in range(8):
            if h % 2:
                act(Obf8[0:qn, h, :], Oacc[up][0:qn, h, 0:256], AF.Copy, [("O", up, h), "linv8"], [("Obf", h)], scale=linv8[0:qn, h:h + 1])
            else:
                ts(Obf8[0:qn, h, :], Oacc[up][0:qn, h, 0:256], linv8[0:qn, h:h + 1], ALU.mult, [("O", up, h), "linv8"], [("Obf", h)])
        for hg in range(2):
            bt = 7
            for hh in range(4):
                h = hg * 4 + hh
                for c in range(2):
                    tr(PSB[bt][:, (hh * 2 + c) * 128:(hh * 2 + c) * 128 + qn], Obf8[0:qn, h, c * 128:(c + 1) * 128], identb[0:qn, 0:qn],
                       [("Obf", h), "identb"], [("ps", bt)])
            cp(abuf[:, 8 * hg:8 * hg + 8, qc0:qc0 + qn], PSB[bt][:, 0:1024].rearrange("p (c q) -> p c q", c=8)[:, :, 0:qn],
               [("ps", bt)], [("a", 8 * hg + i_) for i_ in range(8)], eng=("act" if hg else "dve"))

    for ti, (t0, T, j) in enumerate(tiles):
        xb = ti % 2
        load_x(xb, XB, t0, T, ("XB", ti))
        dma(cqn[:, :, 0:T], cqn_s.ap()[:, :, t0:t0 + T], [("cqn_s", ti)], ["cqn"], q="sp")
        dma(ropeCt[:, 0:T], ropeC[:, t0:t0 + T], [], ["ropeC"], q="sp")
        dma(ropeSt[:, 0:T], ropeS[:, t0:t0 + T], [], ["ropeS"], q="sp")
        for h in range(8):
            sv, sk = wload(uq_v[:, :, h * 256:(h + 1) * 256], [128, 3, 256], "b_uq")
            uk_v, uk_k = wload(ukT_v[:, h, :], [128, 256], "b_ukT")
            bn, b1, b2 = P.bank(), P.bank(), P.bank()
            for k in range(3):
                mm(PS[bn][:, 0:T], sv[:, k, 0:128], cqn[:, k, 0:T], k == 0, k == 2, [sk, "cqn"], [("ps", bn)])
            for k in range(3):
                mm(PS[b1][0:64, 0:T], sv[:, k, 128:192], cqn[:, k, 0:T], k == 0, k == 2, [sk, "cqn"], [("ps", b1)])
            for k in range(3):
                mm(PS[b2][0:64, 0:T], sv[:, k, 192:256], cqn[:, k, 0:T], k == 0, k == 2, [sk, "cqn"], [("ps", b2)])
            qn_ = qnope[h % 2]
            cp(qn_[:, 0:T], PS[bn][:, 0:T], [("ps", bn)], [("qnope", h % 2)], eng="act")
            tt(Ssb[0][0:64, 0:T], PS[b1][0:64, 0:T], ropeCt[:, 0:T], ALU.mult, [("ps", b1), "ropeC"], [("Ssb", 0)])
            tt(Ssb[1][0:64, 0:T], PS[b2][0:64, 0:T], ropeSt[:, 0:T], ALU.mult, [("ps", b2), "ropeS"], [("Ssb", 1)])
            tt(qpe[0:64, h, 0:T], Ssb[0][0:64, 0:T], Ssb[1][0:64, 0:T], ALU.add, [("Ssb", 0), ("Ssb", 1)], [("qpe", h)])
            for c in range(2):
                b = P.bank()
                mm(PS[b][:, 0:T], uk_v[:, c * 128:(c + 1) * 128], qn_[:, 0:T], True, True, [uk_k, ("qnope", h % 2)], [("ps", b)])
                cp(qlat[:, 2 * h + c, 0:T], PS[b][:, 0:T], [("ps", b)], [("qlat", h)], eng=("act" if c else "dve"))
        if j >= 0:
            nseg_k = 2 * j + 2
            pairs = [(u, s_) for u in range(4) for s_ in range(nseg_k)]

            def kvload(pi):
                u_, s_ = pairs[pi]
                rr, jj = s_ % 2, s_ // 2
                kvs = (kv_n[0] + pi) % 3
                dma(KTs[kvs][:, :, :], kxk_out[jj].ap()[rr * 384:(rr + 1) * 384, :].rearrange("(c p) t -> p c t", p=128),
                    [("kxk_out", jj)], [("KTs", kvs)], q="sp")
                dma(Vs[kvs][:, :, 0:256], kxv_out[jj].ap()[rr * SEG:(rr + 1) * SEG, :].rearrange("(k p) c -> p k c", p=128),
                    [("kxv_out", jj)], [("Vs", kvs)], q="sp")

            units = []
            for u in range(4):
                up = u % 2
                for s_ in range(nseg_k):
                    pi = u * nseg_k + s_
                    kvs = (kv_n[0] + pi) % 3
                    mi = s_ - 2 * j
                    mt = None
                    if mi >= 0:
                        mt = mtile[mi]
                    for h in range(8):
                        pre = None
                        post = None
                        if h == 0:
                            def pre(pi=pi, mi=mi, u=u, s_=s_, up=up):
                                if s_ == 0:
                                    att_init(up)
                                if pi == 0:
                                    kvload(0)
                                if pi + 1 < len(pairs):
                                    kvload(pi + 1)
                                if mi >= 0:
                                    dma(mtile[mi][:, :], amask[mi, u], [], [("mtile", mi)], q="sp")
                        if h == 7 and s_ == nseg_k - 1:
                            def post(u=u, up=up):
                                att_fin(up, 128, u * 128)
                        nkb_ = 4
                        kn_u = nkb_ * 128
                        units.append(att_unit(h, up, s_, 128, u * 128,
                                              lambda c, kvs=kvs, kn_u=kn_u: KTs[kvs][:, c, 0:kn_u], ("KTs", kvs), kn_u,
                                              [(Vs[kvs][:, kb, :], 128, ("Vs", kvs)) for kb in range(nkb_)],
                                              None if mt is None else mt[:, 0:kn_u], None if mt is None else ("mtile", mi),
                                              pre=pre, post=post))
            run_units(units)
            kv_n[0] += len(pairs)
        else:
            barrier()
            memset(KTp[64:128, 2, :], 0.0, ["KTp"])
            memset(KTn[64:128, 2, :], 0.0, ["KTn"])
            memset(Vp[:, :, 256:264], 1.0, ["Vp"])
            memset(Vn[:, 256:264], 1.0, ["Vn"])
            for sq_ in range(NSAMP):
                dma(Vp[:, :, 0:256], cckv[sq_].rearrange("(k p) c -> p k c", p=128), [], ["Vp"], q="pool")
                dma(kpp[:, :, :], ckpe[sq_].rearrange("(k p) c -> p k c", p=128), [], ["kpp"], q="pool")
                for c in range(2):
                    bt = P.bank()
                    for kb in range(8):
                        tr(PSB[bt][:, kb * 128:(kb + 1) * 128], Vp[:, kb, c * 128:(c + 1) * 128], identb[:, :],
                           ["Vp", "identb"], [("ps", bt)])
                    cp(KTp[:, c, :], PSB[bt][:, 0:1024], [("ps", bt)], ["KTp"], eng=("act" if c else "dve"))
                bt = P.bank()
                for kb in range(8):
                    tr(PSB[bt][0:64, kb * 128:(kb + 1) * 128], kpp[:, kb, :], identb[:, :], ["kpp", "identb"], [("ps", bt)])
                cp(KTp[0:64, 2, :], PSB[bt][0:64, 0:1024], [("ps", bt)], ["KTp"])
                dma(KTn[:, 0:2, :], ksk.ap()[0:256, sq_ * SL:(sq_ + 1) * SL].rearrange("(c p) t -> p c t", p=128), ["ksk"], ["KTn"], q="sp")
                dma(KTn[0:64, 2, :], ksk.ap()[256:320, sq_ * SL:(sq_ + 1) * SL], ["ksk"], ["KTn"], q="sp")
                dma(Vn[:, 0:256], ksv.ap()[sq_ * SL:(sq_ + 1) * SL, :], ["ksv"], ["Vn"], q="sp")
                up = sq_ % 2
                att_init(up)
                units = []
                for pb in range(2):
                    for h in range(8):
                        units.append(att_unit(h, up, pb, 64, sq_ * SL, lambda c, pb=pb: KTp[:, c, pb * 512:(pb + 1) * 512], "KTp", 512,
                                              [(Vp[:, pb * 4 + kb, :], 128, "Vp") for kb in range(4)], None, None))
                for h in range(8):
                    units.append(att_unit(h, up, 2, 64, sq_ * SL, lambda c: KTn[:, c, :], "KTn", SL, [(Vn[:, :], 64, "Vn")], None, None))
                run_units(units)
                att_fin(up, 64, sq_ * SL)
        svv, svk = wload(uv_v, [128, 2, 1024], "b_uv")
        for h in range(8):
            b = P.bank()
            for c in range(2):
                mm(PS[b][:, 0:T], svv[:, c, h * 128:(h + 1) * 128], abuf[:, 2 * h + c, 0:T], c == 0, c == 1,
                   [svk, ("a", 2 * h + c)], [("ps", b)])
            cp(hT[:, h, 0:T], PS[b][:, 0:T], [("ps", b)], [("h", h)], eng=("act" if h % 2 else "dve"))
        sls = [wload(mo_v[:, 2 * f2:2 * f2 + 2, :], [128, 2, 1024], "b_mo") for f2 in range(4)]
        for m in range(8):
            b = P.bank()
            for k in range(8):
                sv2, sk2 = sls[k // 2]
                mm(PS[b][:, 0:T], sv2[:, k % 2, m * 128:(m + 1) * 128], hT[:, k, 0:T], k == 0, k == 7,
                   [sk2, ("h", k)], [("ps", b)])
            tt(xT[xb][:, m, 0:T], PS[b][:, 0:T], xT[xb][:, m, 0:T], ALU.add, [("ps", b), xk(xb, m)], [xk(xb, m)])
        ffn(xb, T, 1, 1, 5)
        norm_x(xb, T, 6, dst_fp32=True)
        nu = T // 128
        for u in range(nu):
            for kg in range(2):
                b = P.bank()
                for kk in range(4):
                    k = kg * 4 + kk
                    tr(PS[b][:, kk * 128:(kk + 1) * 128], xT[xb][:, k, u * 128:(u + 1) * 128], ident[:, :],
                       [xk(xb, k), "ident"], [("ps", b)])
                cp(stage[:, u, kg * 512:(kg + 1) * 512], PS[b][:, :], [("ps", b)], STG, eng=("act" if kg else "dve"))
        dma(y_out[t0:t0 + T, :].rearrange("(u p) d -> p u d", p=128), stage[:, 0:nu, :], STG, [("y_out", ti)], q="pool")

    barrier()

    P.finalize()
    nsig = {e: sum(1 for o in P.q[e] if (not o.dma) and o.sig) for e in ("pe", "act", "dve", "pool")}
    sems = {e: [nc.alloc_semaphore(f"s_{e}{i}") for i in range(nsig[e] // SEMCH + 1)] for e in ("pe", "act", "dve", "pool")}
    lanes = {(q, i): nc.alloc_semaphore(f"l_{q}{i}") for q in ("sp", "pool", "act") for i in range(NLANE)}
    with nc.Block() as block:
        @block.sync
        def _(e):
            P.emit("sp", e, sems, lanes)

        @block.tensor
        def _(e):
            P.emit("pe", e, sems, lanes)

        @block.scalar
        def _(e):
            P.emit("act", e, sems, lanes)

        @block.vector
        def _(e):
            P.emit("dve", e, sems, lanes)

        @block.gpsimd
        def _(e):
            P.emit("pool", e, sems, lanes)
    es.close()
    return nc, P


_CACHE = {}


def _rope_tables(pos):
    half = 32
    inv = (np.float32(10000.0) ** (-np.arange(half, dtype=np.float32) / np.float32(half))).astype(np.float32)
    ang = (pos.astype(np.float32)[:, None] * inv[None, :]).astype(np.float32)
    cos, sin = np.cos(ang).astype(np.float32), np.sin(ang).astype(np.float32)
    C = np.concatenate([cos, cos], axis=1).T
    S = np.concatenate([-sin, sin], axis=1).T
    return np.ascontiguousarray(C), np.ascontiguousarray(S)


def kernel(x_prompt, x_sample, state_gla, cache_ckv, cache_kpe,
           norm_ffn1, w_ffn1_gate, w_ffn1_up, w_ffn1_down, norm_mix,
           norm_ffn2, w_ffn2_gate, w_ffn2_up, w_ffn2_down,
           w_gla_in, w_gla_gk_up, b_gla_gk, gla_out_norm, w_gla_out,
           w_mla_down, mla_q_norm, w_mla_uq, mla_kv_norm, w_mla_uk, w_mla_uv, w_mla_out,
           norm_final):
    f = lambda a: np.ascontiguousarray(np.asarray(a, dtype=np.float32))
    x_prompt, x_sample = f(x_prompt), f(x_sample)
    B, L, _ = x_prompt.shape
    NSEG = L // (2 * SEG)
    NTP = NSEG * SEG
    NTOK = NTP + NSAMP * SL
    if NSEG not in _CACHE:
        _CACHE[NSEG] = build(NSEG)[0]
    nc = _CACHE[NSEG]

    wdn = f(w_mla_down)[0]
    perm = np.concatenate([np.arange(32, 64), np.arange(0, 32)])
    w_dn = np.concatenate([wdn, wdn[:, 640 + perm]], axis=1)
    wuq = f(w_mla_uq)[0].reshape(384, 8, 192)
    w_uq = np.concatenate([wuq, wuq[:, :, 128 + perm]], axis=2).reshape(384, 2048)
    w_ukT = f(w_mla_uk)[0].transpose(1, 2, 0).reshape(1024, 256)
    w_uv = f(w_mla_uv)[0].reshape(256, 1024)
    gkw = np.zeros((33, 512), np.float32)
    gkw[0:16] = f(w_gla_gk_up)[0]
    gkw[32] = f(b_gla_gk)[0]
    gl = [f(norm_ffn1)[0], f(norm_mix)[0], f(norm_ffn2)[0], f(norm_ffn1)[1], f(norm_mix)[1], f(norm_ffn2)[1], f(norm_final)]
    gains = np.concatenate([g.reshape(8, 128).T for g in gl], axis=1)
    U = np.triu(np.ones((64, 64), np.float32))
    Lm = np.tril(np.ones((64, 64), np.float32), -1)
    tri = np.concatenate([U, Lm], axis=1) * np.float32(-1.0 / 16.0)
    shared = {
        "w_gin": f(w_gla_in)[0], "w_gout": f(w_gla_out)[0], "gkw": gkw, "w_dn": np.ascontiguousarray(w_dn),
        "w_uq": np.ascontiguousarray(w_uq), "w_ukT": np.ascontiguousarray(w_ukT), "w_uv": np.ascontiguousarray(w_uv),
        "w_mo": f(w_mla_out)[0], "gains": np.ascontiguousarray(gains),
        "qn_g": np.ascontiguousarray(f(mla_q_norm)[0].reshape(3, 128).T),
        "kvn_g": np.ascontiguousarray(f(mla_kv_norm)[0].reshape(2, 128).T),
        "gon": f(gla_out_norm).reshape(1, 256), "ident": np.eye(128, dtype=np.float32),
        "tri": np.ascontiguousarray(tri), "umask": U,
    }
    for l in range(2):
        shared[f"wg{l}0"], shared[f"wu{l}0"], shared[f"wd{l}0"] = f(w_ffn1_gate)[l], f(w_ffn1_up)[l], f(w_ffn1_down)[l]
        shared[f"wg{l}1"], shared[f"wu{l}1"], shared[f"wd{l}1"] = f(w_ffn2_gate)[l], f(w_ffn2_up)[l], f(w_ffn2_down)[l]
    state_gla, cache_ckv, cache_kpe = f(state_gla), f(cache_ckv), f(cache_kpe)
    in_maps = []
    for c in range(8):
        p, r = c // 2, c % 2
        segs = [x_prompt[p, (2 * j + r) * SEG:(2 * j + r + 1) * SEG] for j in range(NSEG)]
        xs = x_sample[NSAMP * c:NSAMP * (c + 1)].reshape(NSAMP * SL, D)
        x_in = np.concatenate(segs + [xs], axis=0)
        pos = np.concatenate([np.arange((2 * j + r) * SEG, (2 * j + r + 1) * SEG) for j in range(NSEG)]
                             + [PAST + np.arange(SL)] * NSAMP)
        C, S = _rope_tables(pos)
        am = np.zeros((2, 4, 128, 512), np.float32)
        qchunk = (np.arange(512) // 64).reshape(4, 128)
        kchunk = np.arange(512) // 64
        diag = np.where(kchunk[None, None, :] <= qchunk[:, :, None], 0.0, NEG).astype(np.float32)
        if r == 0:
            am[0] = diag
            am[1] = NEG
        else:
            am[0] = 0.0
            am[1] = diag
        m = dict(shared)
        m.update({
            "x_in": np.ascontiguousarray(x_in), "st_in": np.ascontiguousarray(state_gla[0, NSAMP * c:NSAMP * (c + 1)]),
            "cckv": np.ascontiguousarray(cache_ckv[0, NSAMP * c:NSAMP * (c + 1)]),
            "ckpe": np.ascontiguousarray(cache_kpe[0, NSAMP * c:NSAMP * (c + 1)]),
            "ropeC": C, "ropeS": S, "amask": am,
            "rsel": np.ascontiguousarray(np.tile(np.array([[1.0 - r, float(r)]], np.float32), (128, 1))),
        })
        in_maps.append(m)
    res = run_bass_kernel_spmd(nc, in_maps, core_ids=list(range(8)))
    R = res.results
    y_p = np.zeros((B, L, D), np.float32)
    ckv_p = np.zeros((1, B, L, 256), np.float32)
    kpe_p = np.zeros((1, B, L, 64), np.float32)
    y_s = np.zeros((8 * NSAMP, SL, D), np.float32)
    ckv_s = np.zeros((1, 8 * NSAMP, SL, 256), np.float32)
    kpe_s = np.zeros((1, 8 * NSAMP, SL, 64), np.float32)
    gla_p = np.zeros((1, B, 4, 128, 256), np.float32)
    gla_s = np.zeros((1, 8 * NSAMP, 4, 128, 256), np.float32)
    for c in range(8):
        p, r = c // 2, c % 2
        o = R[c]
        for j in range(NSEG):
            g0 = (2 * j + r) * SEG
            y_p[p, g0:g0 + SEG] = o["y_out"][j * SEG:(j + 1) * SEG]
            ckv_p[0, p, g0:g0 + SEG] = o["ckv_out"][j * SEG:(j + 1) * SEG]
            kpe_p[0, p, g0:g0 + SEG] = o["kpe_out"][j * SEG:(j + 1) * SEG]
        y_s[NSAMP * c:NSAMP * (c + 1)] = o["y_out"][NTP:].reshape(NSAMP, SL, D)
        ckv_s[0, NSAMP * c:NSAMP * (c + 1)] = o["ckv_out"][NTP:].reshape(NSAMP, SL, 256)
        kpe_s[0, NSAMP * c:NSAMP * (c + 1)] = o["kpe_out"][NTP:].reshape(NSAMP, SL, 64)
        if r == 0:
            gla_p[0, p] = o["gst_p"].reshape(128, 4, 256).transpose(1, 0, 2)
        gla_s[0, NSAMP * c:NSAMP * (c + 1)] = o["gst_s"].reshape(NSAMP, 128, 4, 256).transpose(0, 2, 1, 3)
    return (y_p, y_s, gla_p, ckv_p, kpe_p, gla_s, ckv_s, kpe_s)
```

```python
import numpy as np
import ml_dtypes
import concourse.bass as bass
import concourse.mybir as mybir
from concourse.bass_utils import run_bass_kernel_spmd

F32 = mybir.dt.float32
BF16 = mybir.dt.bfloat16
AF = mybir.ActivationFunctionType
ALU = mybir.AluOpType
AX = mybir.AxisListType

D = 1024
DFF = 2816
NF = DFF // 128
EPS = 1e-6
SEG = 512
NSAMP = 4
SL = 64
PAST = 1024
MLA_SCALE = float((128 + 64) ** -0.5)
NEG = -30000.0
NLANE = 8
SEMCH = 8000


class Op:
    __slots__ = ("eng", "fn", "deps", "dma", "sig", "sidx", "lane", "lval", "cc")


class Prog:
    def __init__(self):
        self.q = {e: [] for e in ("pe", "act", "dve", "pool", "sp")}
        self.lastw = {}
        self.readers = {}
        self.nbank = 0
        self.epoch = None

    def bank(self):
        b = self.nbank % 8
        self.nbank += 1
        return b

    def add(self, eng, fn, r=(), w=(), dma=False, cc=None):
        op = Op()
        op.eng, op.fn, op.dma, op.sig, op.cc = eng, fn, dma, False, cc
        deps = {}
        barrier = "EPOCH" in w
        w = tuple(k for k in w if k != "EPOCH")
        if self.epoch is not None:
            deps[id(self.epoch)] = self.epoch
        if barrier:
            for e, ops in self.q.items():
                nd = 0
                seen_c = False
                for o in reversed(ops):
                    if o.dma:
                        if o.cc is not None or nd < NLANE:
                            deps[id(o)] = o
                            if o.cc is None:
                                nd += 1
                    elif not seen_c:
                        deps[id(o)] = o
                        seen_c = True
                    if seen_c and nd >= NLANE:
                        break
        for k in r:
            lw = self.lastw.get(k)
            if lw is not None:
                deps[id(lw)] = lw
        for k in w:
            lw = self.lastw.get(k)
            if lw is not None:
                deps[id(lw)] = lw
            for ek, rd in self.readers.get(k, {}).items():
                if ek == "dma":
                    for o in rd:
                        deps[id(o)] = o
                else:
                    deps[id(rd)] = rd
        for k in r:
            rdd = self.readers.setdefault(k, {})
            if dma:
                rdd.setdefault("dma", []).append(op)
            else:
                rdd[eng] = op
        for k in w:
            self.lastw[k] = op
            self.readers[k] = {}
        dl = []
        for d in deps.values():
            if d is op:
                continue
            if (not d.dma) and (not dma) and d.eng == "pe" and eng == "pe":
                continue
            if not d.dma:
                d.sig = True
            dl.append(d)
        op.deps = dl
        self.q[eng].append(op)
        if barrier:
            self.epoch = op
        return op

    def finalize(self):
        for e, ops in self.q.items():
            n = 0
            k = 0
            for op in ops:
                if op.dma:
                    if op.cc is None:
                        op.lane = (e, k % NLANE)
                        op.lval = 16 * (k // NLANE + 1)
                        k += 1
                else:
                    if op.sig:
                        op.sidx = n
                        n += 1

    def emit(self, e, eng, sems, lanes):
        waited = {}

        def need(sem, val):
            key = id(sem)
            if waited.get(key, 0) < val:
                eng.wait_ge(sem, val)
                waited[key] = val

        for op in self.q[e]:
            for d in op.deps:
                if d.dma:
                    if d.cc is not None:
                        need(d.cc, 1)
                    else:
                        need(lanes[d.lane], d.lval)
                else:
                    need(sems[d.eng][d.sidx // SEMCH], d.sidx % SEMCH + 1)
            if op.dma:
                if op.cc is not None:
                    op.fn(eng).then_inc(op.cc)
                else:
                    if op.lval > 16:
                        need(lanes[op.lane], op.lval - 16)
                    op.fn(eng).then_inc(lanes[op.lane], 16)
            else:
                inst = op.fn(eng)
                if op.sig:
                    inst.then_inc(sems[e][op.sidx // SEMCH], 1)


def build(NSEG, debug_phase=9):
    NTP = NSEG * SEG
    NTOK = NTP + NSAMP * SL
    nc = bass.Bass("TRN2", target_bir_lowering=False)
    P = Prog()

    def din(name, shape, dt=F32):
        return nc.dram_tensor(name, list(shape), dt, kind="ExternalInput").ap()

    def dout(name, shape, dt=F32):
        return nc.dram_tensor(name, list(shape), dt, kind="ExternalOutput").ap()

    def dscr(name, shape, dt):
        return nc.dram_tensor(name, list(shape), dt)

    x_in = din("x_in", [NTOK, D])
    st_in = din("st_in", [NSAMP, 4, 128, 256])
    cckv = din("cckv", [NSAMP, PAST, 256])
    ckpe = din("ckpe", [NSAMP, PAST, 64])
    wfg = [[din(f"wg{l}{i}", [D, DFF]) for i in range(2)] for l in range(2)]
    wfu = [[din(f"wu{l}{i}", [D, DFF]) for i in range(2)] for l in range(2)]
    wfd = [[din(f"wd{l}{i}", [DFF, D]) for i in range(2)] for l in range(2)]
    w_gin = din("w_gin", [D, 3088])
    w_gout = din("w_gout", [D, D])
    gkw = din("gkw", [33, 512])
    w_dn = din("w_dn", [D, 768])
    w_uq = din("w_uq", [384, 2048])
    w_ukT = din("w_ukT", [D, 256])
    w_uv = din("w_uv", [256, D])
    w_mo = din("w_mo", [D, D])
    gains = din("gains", [128, 56])
    qn_g = din("qn_g", [128, 3])
    kvn_g = din("kvn_g", [128, 2])
    gon = din("gon", [1, 256])
    ropeC = din("ropeC", [64, NTOK])
    ropeS = din("ropeS", [64, NTOK])
    amask = din("amask", [2, 4, 128, 512])
    rsel = din("rsel", [128, 2])
    ident_in = din("ident", [128, 128])
    tri_in = din("tri", [64, 128])
    umask_in = din("umask", [64, 64])

    y_out = dout("y_out", [NTOK, D])
    ckv_out = dout("ckv_out", [NTOK, 256])
    kpe_out = dout("kpe_out", [NTOK, 64])
    gst_p = dout("gst_p", [128, 1024])
    gst_s = dout("gst_s", [NSAMP, 128, 1024])

    bwg = [[dscr(f"bwg{l}{i}", [D, DFF], BF16) for i in range(2)] for l in range(2)]
    bwu = [[dscr(f"bwu{l}{i}", [D, DFF], BF16) for i in range(2)] for l in range(2)]
    bwd = [[dscr(f"bwd{l}{i}", [DFF, D], BF16) for i in range(2)] for l in range(2)]
    b_gin = dscr("b_gin", [D, 3088], BF16)
    b_gout = dscr("b_gout", [D, D], BF16)
    b_dn = dscr("b_dn", [D, 768], BF16)
    b_uq = dscr("b_uq", [384, 2048], BF16)
    b_ukT = dscr("b_ukT", [D, 256], BF16)
    b_uv = dscr("b_uv", [256, D], BF16)
    b_mo = dscr("b_mo", [D, D], BF16)
    XA = dscr("XA", [128, 8, NTOK], F32)
    XB = dscr("XB", [128, 8, NTOK], F32)
    gx_in = [dscr(f"gx_in{j}", [128, 1032], F32) for j in range(NSEG)]
    gx_out = [dscr(f"gx_out{j}", [256, 1032], F32) for j in range(NSEG)]
    smine = dscr("smine", [NSEG, 128, 1024], F32)
    NBLK = NSEG * (SEG // 256)
    gsk = dscr("gsk", [NBLK, 64, 4 * 512], BF16)
    gsv = dscr("gsv", [NBLK, 64, 4 * 1024], BF16)
    gsh = dscr("gsh", [NBLK, 64, 4 * 512], BF16)
    gsl = dscr("gsl", [NBLK, 64, 4 * 512], BF16)
    cqn_s = dscr("cqn_s", [128, 3, NTOK], BF16)
    kxk_in = [dscr(f"kxk_in{j}", [384, SEG], BF16) for j in range(NSEG)]
    kxk_out = [dscr(f"kxk_out{j}", [768, SEG], BF16) for j in range(NSEG)]
    kxv_in = [dscr(f"kxv_in{j}", [SEG, 256], BF16) for j in range(NSEG)]
    kxv_out = [dscr(f"kxv_out{j}", [2 * SEG, 256], BF16) for j in range(NSEG)]
    ksk = dscr("ksk", [384, NSAMP * SL], BF16)
    ksv = dscr("ksv", [NSAMP * SL, 256], BF16)

    SB_BASE = 16384 + 512
    cur = [SB_BASE]
    hi = [0]

    def sb(name, shape, dt):
        esz = 4 if dt == F32 else 2
        n = 1
        for s in shape[1:]:
            n *= s
        nbytes = (n * esz + 31) // 32 * 32
        t = nc.alloc_sbuf_tensor_at(name, list(shape), dt, offset=cur[0])
        cur[0] += nbytes
        hi[0] = max(hi[0], cur[0])
        return t

    ident = sb("ident", [128, 128], F32)
    identb = sb("identb", [128, 128], BF16)
    onesb = sb("onesb", [128, 128], BF16)
    tri = sb("tri", [64, 128], F32)
    trib = sb("trib", [64, 128], BF16)
    umask = sb("umaskt", [64, 64], F32)
    gain_t = sb("gain_t", [128, 56], F32)
    qng = sb("qng", [128, 3], F32)
    kvng = sb("kvng", [128, 2], F32)
    rselt = sb("rselt", [128, 2], F32)
    epst = sb("epst", [128, 1], F32)
    gonb = sb("gonb", [64, 1024], F32)
    gkwf = sb("gkwf", [33, 512], F32)
    gkwb = sb("gkwb", [33, 512], BF16)
    xT = [sb(f"xT{i}", [128, 8, SEG], F32) for i in range(2)]
    hT = sb("hT", [128, 8, SEG], BF16)
    abuf = sb("abuf", [128, NF, SEG], BF16)
    RS = 6
    ring = sb("ring", [128, RS, 2048], BF16)
    stage_off = cur[0]
    stage = sb("stage", [128, 4, D], F32)
    rstd = sb("rstd", [128, SEG], F32)
    sgt = [sb(f"sgt{i}", [128, SEG], F32) for i in range(2)]
    scr_off = cur[0]
    scr = sb("scr", [128, 6, SEG], F32)
    st64 = sb("st64", [128, 4, 64], F32)
    stv = sb("stv", [128, 4, 256], F32)
    stvb = sb("stvb", [128, 4, 256], BF16)
    kbf = sb("kbf", [128, 3, SEG], BF16)
    scr_end = cur[0]
    ropeCt = sb("ropeCt", [64, SEG], F32)
    ropeSt = sb("ropeSt", [64, SEG], F32)
    bar_t = sb("bar_t", [128, 8], F32)
    groups = [[0, 1], [2, 3], [4, 5], [6, 7]]

    def allgather(src, dst, rkey, wkey):
        sem = nc.alloc_semaphore(f"cc_{wkey[0]}{wkey[1]}")
        P.add("pool", lambda e, s_=src, d_=dst: e.collective_compute("AllGather", ALU.bypass, replica_groups=groups,
                                                                      ins=[s_.ap().opt()], outs=[d_.ap().opt()]),
              [rkey], [wkey], dma=True, cc=sem)

    def barrier():
        P.add("pool", lambda e: e.memset(bar_t[:, :], 0.0), [], ["EPOCH"])
    arena0 = cur[0]
    GT = 256
    NCH = GT // 64
    k_tm = sb("k_tm", [64, NCH, 512], BF16)
    v_tm = sb("v_tm", [64, NCH, 1024], BF16)
    gg_tm = sb("gg_tm", [64, NCH, 1024], BF16)
    la_hi = sb("la_hi", [64, NCH, 512], BF16)
    la_lo = sb("la_lo", [64, NCH, 512], BF16)
    laf = sb("laf", [64, 512], F32)
    qTt = sb("qTt", [128, 4, GT], BF16)
    kTt = sb("kTt", [128, 4, GT], BF16)
    gkT = sb("gkT", [33, GT], BF16)
    e1t = sb("e1t", [64, 512], F32)
    ebt2 = [sb(f"ebt{i}", [128, 4, 64], F32) for i in range(2)]
    enbt2 = [sb(f"enbt{i}", [128, 4, 64], F32) for i in range(2)]
    etl2 = [sb(f"etl{i}", [64, 512], F32) for i in range(2)]
    qdec2 = [sb(f"qdec{i}", [128, 4, 64], BF16) for i in range(2)]
    kinv2 = [sb(f"kinv{i}", [128, 4, 64], BF16) for i in range(2)]
    ktail2 = [sb(f"ktail{i}", [64, 512], BF16) for i in range(2)]
    sTm2 = [sb(f"sTm{i}", [64, 4, 64], BF16) for i in range(2)]
    Sst = sb("Sst", [128, 4, 256], F32)
    Sbf = sb("Sbf", [128, 4, 256], BF16)
    Dacc = sb("Dacc", [128, 8], F32)
    ssq2 = [sb(f"ssq{i}", [64, 4], F32) for i in range(2)]
    ogt2 = [sb(f"ogt{i}", [64, 1024], BF16) for i in range(2)]
    ogT = sb("ogT", [128, 8, GT], BF16)
    gla_end = cur[0]
    cur[0] = stage_off
    gxb = [sb("gxb0", [128, 1032], F32)] * 2
    Rch = sb("Rch", [128, 1024], F32)
    Smt = sb("Smt", [128, 1024], F32)
    assert cur[0] <= stage_off + 16384
    cur[0] = arena0
    cqn = sb("cqn", [128, 3, SEG], BF16)
    qnope = [sb(f"qnope{i}", [128, SEG], BF16) for i in range(2)]
    qlat = sb("qlat", [128, 16, SEG], BF16)
    qpe = sb("qpe", [128, 8, SEG], BF16)
    nmr = sb("nmr", [128, 8, 2, 2], F32)
    smt = [sb(f"smt{i}", [128, 8], F32) for i in range(3)]
    kts_off = cur[0]
    KTs = [sb(f"KTs{i}", [128, 3, SEG], BF16) for i in range(3)]
    Vs = [sb(f"Vs{i}", [128, 4, 264], BF16) for i in range(3)]
    kts_end = cur[0]
    Ssb = [sb(f"Ssb{i}", [128, SEG], F32) for i in range(2)]
    Pbf = [sb(f"Pbf{i}", [128, SEG], BF16) for i in range(3)]
    PTs = [sb(f"PTs{i}", [128, 4, 128], BF16) for i in range(2)]
    Obf8 = sb("Obf8", [128, 8, 256], BF16)
    linv8 = sb("linv8", [128, 8], F32)
    _save = cur[0]
    cur[0] = kts_off
    Vp = sb("Vp", [128, 8, 264], BF16)
    kpp = sb("kpp", [128, 8, 64], BF16)
    KTp = sb("KTp", [128, 3, PAST], BF16)
    KTn = sb("KTn", [128, 3, SL], BF16)
    Vn = sb("Vn", [64, 264], BF16)
    assert cur[0] <= kts_end
    cur[0] = _save
    mla_end = cur[0]
    cur[0] = scr_off
    Oacc = [sb(f"Oacc{i}", [128, 8, 264], F32) for i in range(2)]
    mtile = [sb(f"mtile{i}", [128, SEG], F32) for i in range(2)]
    assert cur[0] <= scr_end
    cur[0] = max(mla_end, gla_end)
    assert hi[0] <= 229376 - 256, hi[0]

    import contextlib
    es = contextlib.ExitStack()
    PSg = [es.enter_context(nc.psum_tensor(f"psb{i}", [128, 512], F32)) for i in range(8)]
    PS = PSg
    PSB = [p.bitcast(BF16) for p in PSg]

    def mm(out, lhsT, rhs, start, stop, r, w):
        P.add("pe", lambda e, o=out, l=lhsT, rh=rhs, s=start, t=stop: e.matmul(o, l, rh, start=s, stop=t), r, w)

    def tr(out, in_, idn, r, w):
        P.add("pe", lambda e, o=out, i=in_, d=idn: e.transpose(o, i, d), r, w)

    def act(out, in_, func, r, w, bias=None, scale=None, accum=None):
        kw = {}
        if bias is not None:
            kw["bias"] = bias
        if scale is not None:
            kw["scale"] = scale
        if accum is not None:
            kw["accum_out"] = accum
        P.add("act", lambda e, o=out, i=in_, f=func, k=kw: e.activation(o, i, f, **k), r, w)

    def tt(out, a, b, op, r, w, eng="dve"):
        P.add(eng, lambda e, o=out, x=a, y=b, p=op: e.tensor_tensor(o, x, y, p), r, w)

    def stt(out, in0, scalar, in1, op0, op1, r, w, accum=None):
        if accum is None:
            P.add("dve", lambda e, o=out, a=in0, s=scalar, b=in1, p0=op0, p1=op1: e.scalar_tensor_tensor(o, a, s, b, p0, p1), r, w)
        else:
            P.add("dve", lambda e, o=out, a=in0, s=scalar, b=in1, p0=op0, p1=op1, ac=accum: e.scalar_tensor_tensor(o, a, s, b, p0, p1, accum_out=ac), r, w)

    def ts(out, in0, s1, op0, r, w, s2=None, op1=None, eng="dve"):
        if op1 is None:
            P.add(eng, lambda e, o=out, a=in0, s=s1, p=op0: e.tensor_scalar(o, a, s, None, p), r, w)
        else:
            P.add(eng, lambda e, o=out, a=in0, s=s1, q=s2, p=op0, p1=op1: e.tensor_scalar(o, a, s, q, p, p1), r, w)

    def cp(out, in_, r, w, eng="dve"):
        if eng == "act":
            P.add("act", lambda e, o=out, i=in_: e.copy(o, i), r, w)
        else:
            P.add(eng, lambda e, o=out, i=in_: e.tensor_copy(o, i), r, w)

    def memset(ap, val, w, eng="dve"):
        P.add(eng, lambda e, a=ap, v=val: e.memset(a, v), (), w)

    def dma(out, in_, r, w, q="sp"):
        P.add(q, lambda e, o=out, i=in_: e.dma_start(out=o, in_=i), r, w, dma=True)

    ring_n = [0]
    STG = [("stg", 0), ("stg", 1)]
    stg_n = [0]
    stage_flat = stage[:, :, :].rearrange("p u d -> p (u d)")

    stg_views = [stage_flat[:, 0:2048], stage_flat[:, 2048:4096],
                 xT[1][:, 0:4, :].rearrange("p k t -> p (k t)"), xT[1][:, 4:8, :].rearrange("p k t -> p (k t)")]
    stg_keys = [[("stg", 0)], [("stg", 1)], [("x", 1, k) for k in range(4)], [("x", 1, k) for k in range(4, 8)]]
    pending_wb = []

    def wload_first(src_f32, dst_bf16, shape, wkey):
        sg = stg_n[0] % 4
        stg_n[0] += 1
        s_ = ring_n[0] % RS
        ring_n[0] += 1
        n = 1
        for d_ in shape[1:]:
            n *= d_
        sv = stg_views[sg][:, 0:n]
        rv = ring[:, s_, 0:n]
        if len(shape) == 3:
            sv3 = sv.rearrange("p (a b) -> p a b", a=shape[1])
            rv3 = rv.rearrange("p (a b) -> p a b", a=shape[1])
        else:
            sv3, rv3 = sv, rv
        dma(sv3, src_f32, [], stg_keys[sg], q="sp")
        cp(rv, sv, stg_keys[sg], [("ring", s_)], eng=("act" if sg % 2 else "dve"))
        pending_wb.append((dst_bf16, rv3, ("ring", s_), wkey))
        while len(pending_wb) > 3:
            d_, r_, rk_, wk_ = pending_wb.pop(0)
            dma(d_, r_, [rk_], [wk_], q="sp")
        return rv3, ("ring", s_)

    def flush_wb():
        while pending_wb:
            d_, r_, rk_, wk_ = pending_wb.pop(0)
            dma(d_, r_, [rk_], [wk_], q="sp")

    def wload(src_ap, shape, rkey):
        s = ring_n[0] % RS
        ring_n[0] += 1
        n = 1
        for d_ in shape[1:]:
            n *= d_
        flat = ring[0:shape[0], s, 0:n]
        if len(shape) == 3:
            view = flat.rearrange("p (a b) -> p a b", a=shape[1])
        else:
            view = flat
        dma(view, src_ap, [rkey], [("ring", s)])
        return view, ("ring", s)

    dma(ident[:, :], ident_in, [], ["ident"])
    cp(identb[:, :], ident[:, :], ["ident"], ["identb"])
    memset(onesb[:, :], 1.0, ["onesb"])
    memset(epst[:, :], EPS, ["epst"])
    dma(tri[:, :], tri_in, [], ["tri"])
    cp(trib[:, :], tri[:, :], ["tri"], ["trib"])
    dma(umask[:, :], umask_in, [], ["umask"])
    dma(gain_t[:, :], gains, [], ["gains"])
    dma(qng[:, :], qn_g, [], ["gains"])
    dma(kvng[:, :], kvn_g, [], ["gains"])
    dma(rselt[:, :], rsel, [], ["gains"])
    for h in range(4):
        dma(gonb[:, h * 256:(h + 1) * 256], gon.broadcast_to([64, 256]), [], ["gonb"])
    dma(gkwf[:, :], gkw, [], ["gkwf"])
    cp(gkwb[:, :], gkwf[:, :], ["gkwf"], ["gkwb"])
    memset(gkT[:, :], 0.0, ["gkT"])
    memset(gkT[32:33, :], 1.0, ["gkT"])
    memset(kbf[64:128, 2, :], 0.0, [("kbf", 2)])

    conv_jobs = {1: [], 2: [], 3: []}
    cur_grp = [1]

    def cdma(d_ap, s_ap, key):
        conv_jobs[cur_grp[0]].append(lambda d_=d_ap, s_=s_ap, k_=key: dma(d_, s_, [], [k_], q="pool"))

    def conv(dst, src, rows, cols, key):
        a = 1
        while cols // a > 2048 or cols % a:
            a += 1
        RB = 512
        for r0 in range(0, rows, RB):
            r1 = min(rows, r0 + RB)
            s_ap = src[r0:r1, :].rearrange("r (a c) -> r a c", a=a)
            d_ap = dst.ap()[r0:r1, :].rearrange("r (a c) -> r a c", a=a)
            cdma(d_ap, s_ap, key)

    def conv_cols(dst, src, key):
        for fg in range(NF // 2):
            cdma(dst.ap()[:, fg * 256:(fg + 1) * 256], src[:, fg * 256:(fg + 1) * 256], (key, fg))

    def conv_rows(dst, src, key):
        for f2 in range(NF // 2):
            cdma(dst.ap()[f2 * 256:(f2 + 1) * 256, :], src[f2 * 256:(f2 + 1) * 256, :], (key, f2))

    def emit_conv(grp, n):
        jobs = conv_jobs[grp]
        for _ in range(min(n, len(jobs))):
            jobs.pop(0)()

    cur_grp[0] = 1
    conv(b_gin, w_gin, D, 3088, "b_gin")
    conv(b_gout, w_gout, D, D, "b_gout")
    cur_grp[0] = 2
    for (l, i) in ((0, 1), (1, 0)):
        conv_cols(bwg[l][i], wfg[l][i], f"bwg{l}{i}")
        conv_cols(bwu[l][i], wfu[l][i], f"bwu{l}{i}")
        conv_rows(bwd[l][i], wfd[l][i], f"bwd{l}{i}")
    conv(b_dn, w_dn, D, 768, "b_dn")
    cur_grp[0] = 3
    conv(b_uq, w_uq, 384, 2048, "b_uq")
    conv(b_ukT, w_ukT, D, 256, "b_ukT")
    conv(b_uv, w_uv, 256, D, "b_uv")
    conv(b_mo, w_mo, D, D, "b_mo")
    conv_cols(bwg[1][1], wfg[1][1], "bwg11")
    conv_cols(bwu[1][1], wfu[1][1], "bwu11")
    conv_rows(bwd[1][1], wfd[1][1], "bwd11")
    emit_conv(1, 10 ** 6)
    n2_per_tile = -(-len(conv_jobs[2]) // max(1, NSEG))
    n3_per_tile = -(-len(conv_jobs[3]) // max(1, NSEG))

    def rmsnorm(src, dst, garr, Dn, T, np_=128):
        n = len(src)
        for i, (s_ap, s_k) in enumerate(src):
            act(abuf[0:np_, i, 0:T], s_ap, AF.Square, [s_k], [("a", i)])
        b = P.bank()
        for i in range(n):
            mm(PS[b][0:np_, 0:T], onesb[0:np_, 0:np_], abuf[0:np_, i, 0:T], i == 0, i == n - 1,
               [("a", i), "onesb"], [("ps", b)])
        act(rstd[0:np_, 0:T], PS[b][0:np_, 0:T], AF.Ln, [("ps", b), "epst"], ["rstd"],
            bias=epst[0:np_, 0:1], scale=1.0 / Dn)
        act(rstd[0:np_, 0:T], rstd[0:np_, 0:T], AF.Exp, ["rstd"], ["rstd"], scale=-0.5)
        for i, ((s_ap, s_k), (d_ap, d_k)) in enumerate(zip(src, dst)):
            stt(d_ap, s_ap, garr[i], rstd[0:np_, 0:T], ALU.mult, ALU.mult, [s_k, "rstd", "gains"], [d_k])

    def xk(b, k):
        return ("x", b, k)

    def norm_x(xb, T, gi, dst_fp32=False):
        src = [(xT[xb][:, k, 0:T], xk(xb, k)) for k in range(8)]
        if dst_fp32:
            dst = src
        else:
            dst = [(hT[:, k, 0:T], ("h", k)) for k in range(8)]
        rmsnorm(src, dst, [gain_t[:, gi * 8 + k:gi * 8 + k + 1] for k in range(8)], D, T)

    def ffn(xb, T, l, i, gi, first=False):
        norm_x(xb, T, gi)
        wkey = f"{l}{i}"
        gv = bwg[l][i].ap().rearrange("(kc p) n -> p kc n", p=128)
        uv = bwu[l][i].ap().rearrange("(kc p) n -> p kc n", p=128)
        dv = bwd[l][i].ap().rearrange("(f p) n -> p f n", p=128)
        for fg in range(NF // 2):
            if first:
                sg_v, sg_k = wload_first(wfg[l][i].rearrange("(kc p) n -> p kc n", p=128)[:, :, fg * 256:(fg + 1) * 256],
                                         gv[:, :, fg * 256:(fg + 1) * 256], [128, 8, 256], ("bwg" + wkey, fg))
                su_v, su_k = wload_first(wfu[l][i].rearrange("(kc p) n -> p kc n", p=128)[:, :, fg * 256:(fg + 1) * 256],
                                         uv[:, :, fg * 256:(fg + 1) * 256], [128, 8, 256], ("bwu" + wkey, fg))
            else:
                sg_v, sg_k = wload(gv[:, :, fg * 256:(fg + 1) * 256], [128, 8, 256], ("bwg" + wkey, fg))
                su_v, su_k = wload(uv[:, :, fg * 256:(fg + 1) * 256], [128, 8, 256], ("bwu" + wkey, fg))
            for fi in range(2):
                f = 2 * fg + fi
                bg = P.bank()
                bu = P.bank()
                for k in range(8):
                    mm(PS[bg][:, 0:T], sg_v[:, k, fi * 128:(fi + 1) * 128], hT[:, k, 0:T], k == 0, k == 7,
                       [sg_k, ("h", k)], [("ps", bg)])
                for k in range(8):
                    mm(PS[bu][:, 0:T], su_v[:, k, fi * 128:(fi + 1) * 128], hT[:, k, 0:T], k == 0, k == 7,
                       [su_k, ("h", k)], [("ps", bu)])
                st = sgt[f % 2]
                act(st[:, 0:T], PS[bg][:, 0:T], AF.Silu, [("ps", bg)], [("sgt", f % 2)])
                tt(abuf[:, f, 0:T], st[:, 0:T], PS[bu][:, 0:T], ALU.mult, [("sgt", f % 2), ("ps", bu)], [("a", f)])
        NTAIL = 3
        tail = []
        for f2 in range(NF // 2):
            if first:
                sd_v, sd_k = wload_first(wfd[l][i].rearrange("(f p) n -> p f n", p=128)[:, 2 * f2:2 * f2 + 2, :],
                                         dv[:, 2 * f2:2 * f2 + 2, :], [128, 2, 1024], ("bwd" + wkey, f2))
            else:
                sd_v, sd_k = wload(dv[:, 2 * f2:2 * f2 + 2, :], [128, 2, 1024], ("bwd" + wkey, f2))
            if f2 >= NF // 2 - NTAIL:
                tail.append((f2, sd_v, sd_k))
                continue
            for fi in range(2):
                f = 2 * f2 + fi
                for m in range(8):
                    mm(PS[m][:, 0:T], sd_v[:, fi, m * 128:(m + 1) * 128], abuf[:, f, 0:T], f == 0, f == NF - 1,
                       [sd_k, ("a", f)], [("ps", m)])
        if first:
            flush_wb()
        for m in range(8):
            for (f2, sd_v, sd_k) in tail:
                for fi in range(2):
                    f = 2 * f2 + fi
                    mm(PS[m][:, 0:T], sd_v[:, fi, m * 128:(m + 1) * 128], abuf[:, f, 0:T], f == 0, f == NF - 1,
                       [sd_k, ("a", f)], [("ps", m)])
            stt(xT[xb][:, m, 0:T], PS[m][:, 0:T], 0.5, xT[xb][:, m, 0:T], ALU.mult, ALU.add,
                [("ps", m), xk(xb, m)], [xk(xb, m)])

    def load_x_tokmajor(xb, t0, T):
        nu = T // 128
        dma(stage[:, 0:nu, :], x_in[t0:t0 + T, :].rearrange("(u p) d -> p u d", p=128), [], STG)
        for k in range(8):
            b = P.bank()
            for u in range(nu):
                tr(PS[b][:, u * 128:(u + 1) * 128], stage[:, u, k * 128:(k + 1) * 128], ident[:, :],
                   STG + ["ident"], [("ps", b)])
            cp(xT[xb][:, k, 0:T], PS[b][:, 0:T], [("ps", b)], [xk(xb, k)], eng=("act" if k % 2 else "dve"))

    def store_x(xb, dst, t0, T, key):
        dma(dst.ap()[:, :, t0:t0 + T], xT[xb][:, :, 0:T], [xk(xb, k) for k in range(8)], [key], q="pool")

    def load_x(xb, src, t0, T, key):
        dma(xT[xb][:, :, 0:T], src.ap()[:, :, t0:t0 + T], [key], [xk(xb, k) for k in range(8)], q="sp")

    tiles = [(j * SEG, SEG, j) for j in range(NSEG)] + [(NTP, NSAMP * SL, -1)]

    gin_v = b_gin.ap().rearrange("(kc p) n -> p kc n", p=128)
    gout_v = b_gout.ap().rearrange("(f p) n -> p f n", p=128)

    def gla_block(xb, c0, full, seg_j, blk_idx, T):
        cols = slice(c0, c0 + GT)
        hk = [("h", k) for k in range(8)]

        def proj_fm(col0, ncols, dst_fn, scale):
            sv, sk = wload(gin_v[:, :, col0:col0 + ncols], [128, 8, ncols], "b_gin")
            for hi_ in range(ncols // 128):
                b = P.bank()
                for k in range(8):
                    mm(PS[b][:, 0:GT], sv[:, k, hi_ * 128:(hi_ + 1) * 128], hT[:, k, cols], k == 0, k == 7,
                       [sk, ("h", k)], [("ps", b)])
                d_ap, d_k = dst_fn(hi_)
                act(d_ap, PS[b][:, 0:GT], AF.Copy, [("ps", b)], [d_k], scale=scale)

        def proj_tm(col0, dst_fn, silu=False):
            sv, sk = wload(gin_v[:, :, col0:col0 + 256], [128, 8, 256], "b_gin")
            for c in range(NCH):
                b = P.bank()
                for k in range(8):
                    mm(PS[b][0:64, 0:256], hT[:, k, c0 + c * 64:c0 + (c + 1) * 64], sv[:, k, :], k == 0, k == 7,
                       [sk, ("h", k)], [("ps", b)])
                d_ap, d_k, r_extra = dst_fn(c)
                if silu:
                    act(sgt[0][0:64, 0:256], PS[b][0:64, 0:256], AF.Silu, [("ps", b)], [("sgt", 0)])
                    tt(d_ap, sgt[0][0:64, 0:256], r_extra, ALU.mult, [("sgt", 0), "gonb"], [d_k])
                else:
                    cp(d_ap, PS[b][0:64, 0:256], [("ps", b)], [d_k], eng=("act" if c % 2 else "dve"))

        reuse = full and seg_j >= 0
        save = (not full) and seg_j >= 0
        bid = seg_j * (SEG // GT) + blk_idx
        allc = list(range(NCH))
        if reuse:
            dma(k_tm[:, :, :].rearrange("p c n -> p (c n)"), gsk.ap()[bid], [("gs", bid)], [("k_tm", c) for c in allc], q="sp")
            dma(v_tm[:, :, :].rearrange("p c n -> p (c n)"), gsv.ap()[bid], [("gs", bid)], [("v_tm", c) for c in allc], q="sp")
            dma(la_hi[:, :, :].rearrange("p c n -> p (c n)"), gsh.ap()[bid], [("gs", bid)], [("lah", c) for c in allc], q="sp")
            dma(la_lo[:, :, :].rearrange("p c n -> p (c n)"), gsl.ap()[bid], [("gs", bid)], [("lal", c) for c in allc], q="sp")
        if not reuse:
          sv, sk = wload(gin_v[:, :, 3072:3088], [128, 8, 16], "b_gin")
          b = P.bank()
          for k in range(8):
            mm(PS[b][0:16, 0:GT], sv[:, k, :], hT[:, k, cols], k == 0, k == 7, [sk, ("h", k)], [("ps", b)])
          cp(gkT[0:16, :], PS[b][0:16, 0:GT], [("ps", b)], ["gkT"])
          for c in range(NCH):
            b = P.bank()
            mm(PS[b][0:64, :], gkT[0:33, c * 64:(c + 1) * 64], gkwb[0:33, :], True, True, ["gkT", "gkwb"], [("ps", b)])
            act(e1t[:, :], PS[b][0:64, :], AF.Exp, [("ps", b)], ["e1t"], scale=-1.0)
            act(laf[:, :], e1t[:, :], AF.Ln, ["e1t"], ["laf"], bias=1.0)
            cp(la_hi[:, c, :], laf[:, :], ["laf"], [("lah", c)])
            tt(la_lo[:, c, :], laf[:, :], la_hi[:, c, :], ALU.subtract, ["laf", ("lah", c)], [("lal", c)])
        if full:
            for s_ in range(2):
                proj_fm(s_ * 256, 256, lambda hi_, s_=s_: (qTt[:, 2 * s_ + hi_, :], ("qT", 2 * s_ + hi_)), 128 ** -0.5)
            for s_ in range(2):
                proj_fm(512 + s_ * 256, 256, lambda hi_, s_=s_: (kTt[:, 2 * s_ + hi_, :], ("kT", 2 * s_ + hi_)), 1.0)
        if not reuse:
            for s_ in range(2):
                proj_tm(512 + s_ * 256, lambda c, s_=s_: (k_tm[:, c, s_ * 256:(s_ + 1) * 256], ("k_tm", c), None))
            for s_ in range(4):
                proj_tm(1024 + s_ * 256, lambda c, s_=s_: (v_tm[:, c, s_ * 256:(s_ + 1) * 256], ("v_tm", c), None))
        if save:
            dma(gsk.ap()[bid], k_tm[:, :, :].rearrange("p c n -> p (c n)"), [("k_tm", c) for c in allc], [("gs", bid)], q="pool")
            dma(gsv.ap()[bid], v_tm[:, :, :].rearrange("p c n -> p (c n)"), [("v_tm", c) for c in allc], [("gs", bid)], q="pool")
            dma(gsh.ap()[bid], la_hi[:, :, :].rearrange("p c n -> p (c n)"), [("lah", c) for c in allc], [("gs", bid)], q="pool")
            dma(gsl.ap()[bid], la_lo[:, :, :].rearrange("p c n -> p (c n)"), [("lal", c) for c in allc], [("gs", bid)], q="pool")
        if full:
            for s_ in range(4):
                proj_tm(2048 + s_ * 256,
                        lambda c, s_=s_: (gg_tm[:, c, s_ * 256:(s_ + 1) * 256], ("gg_tm", c),
                                          gonb[:, s_ * 256:(s_ + 1) * 256]), silu=True)
        def stageA(c):
            p = c % 2
            ebt, enbt, etl, qdec, kinv, ktail, sTm = ebt2[p], enbt2[p], etl2[p], qdec2[p], kinv2[p], ktail2[p], sTm2[p]
            b = P.bank()
            for h in range(4):
                mm(PS[b][:, h * 64:(h + 1) * 64], la_hi[:, c, h * 128:(h + 1) * 128], trib[:, 0:64], True, False,
                   [("lah", c), "trib"], [("ps", b)])
                mm(PS[b][:, h * 64:(h + 1) * 64], la_lo[:, c, h * 128:(h + 1) * 128], trib[:, 0:64], False, True,
                   [("lal", c), "trib"], [("ps", b)])
            b2 = P.bank()
            mm(PS[b2][0:64, :], trib[:, 64:128], la_hi[:, c, :], True, False, [("lah", c), "trib"], [("ps", b2)])
            mm(PS[b2][0:64, :], trib[:, 64:128], la_lo[:, c, :], False, True, [("lal", c), "trib"], [("ps", b2)])
            act(ebt[:, :, :].rearrange("p h t -> p (h t)"), PS[b][:, 0:256], AF.Exp, [("ps", b)], [("eb", p)])
            if full:
                act(enbt[:, :, :].rearrange("p h t -> p (h t)"), PS[b][:, 0:256], AF.Exp, [("ps", b)], [("enb", p)], scale=-1.0)
            act(etl[:, :], PS[b2][0:64, :], AF.Exp, [("ps", b2)], [("etl", p)])
            if full:
                cc_ = slice(c * 64, (c + 1) * 64)
                tt(qdec[:, :, :], qTt[:, :, cc_], ebt[:, :, :], ALU.mult, [("qT", h) for h in range(4)] + [("eb", p)], [("qdec", p)])
                tt(kinv[:, :, :], kTt[:, :, cc_], enbt[:, :, :], ALU.mult, [("kT", h) for h in range(4)] + [("enb", p)], [("kinv", p)])
                b3 = P.bank()
                for h in range(4):
                    mm(PS[b3][0:64, h * 64:(h + 1) * 64], kinv[:, h, :], qdec[:, h, :], True, True,
                       [("kinv", p), ("qdec", p)], [("ps", b3)])
                tt(sTm[:, :, :], PS[b3][0:64, 0:256].rearrange("p (h t) -> p h t", h=4),
                   umask[:, :].rearrange("p (o t) -> p o t", o=1).broadcast_to([64, 4, 64]), ALU.mult,
                   [("ps", b3), "umask"], [("sTm", p)])
            tt(ktail[:, :], k_tm[:, c, :], etl[:, :], ALU.mult, [("k_tm", c), ("etl", p)], [("ktail", p)])
            if seg_j >= 0 and not full:
                tt(Dacc[:, 0:4], Dacc[:, 0:4], ebt[:, :, 63], ALU.mult, ["Dacc", ("eb", p)], ["Dacc"])

        def stageB1(c):
            p = c % 2
            ebt, qdec, ktail, sTm, ssq, ogt = ebt2[p], qdec2[p], ktail2[p], sTm2[p], ssq2[p], ogt2[p]
            if seg_j < 0:
                sq_ = c
                dma(Sst[:, :, :], st_in[sq_].rearrange("h p v -> p h v"), [], [("S", h) for h in range(4)], q="sp")
                for h in range(4):
                    cp(Sbf[:, h, :], Sst[:, h, :], [("S", h)], [("Sbf", h)], eng="act")
            if full:
                bo = [P.bank(), P.bank()]
                for h in range(4):
                    o_ps = PS[bo[h // 2]][0:64, (h % 2) * 256:(h % 2 + 1) * 256]
                    mm(o_ps, sTm[:, h, :], v_tm[:, c, h * 256:(h + 1) * 256], True, False,
                       [("sTm", p), ("v_tm", c)], [("ps", bo[h // 2])])
                    mm(o_ps, qdec[:, h, :], Sbf[:, h, :], False, True, [("qdec", p), ("Sbf", h)], [("ps", bo[h // 2])])
            bk = [P.bank(), P.bank()]
            for h in range(4):
                mm(PS[bk[h // 2]][:, (h % 2) * 256:(h % 2 + 1) * 256], ktail[:, h * 128:(h + 1) * 128],
                   v_tm[:, c, h * 256:(h + 1) * 256], True, True, [("ktail", p), ("v_tm", c)], [("ps", bk[h // 2])])
            for h in range(4):
                stt(Sst[:, h, :], Sst[:, h, :], ebt[:, h, 63:64], PS[bk[h // 2]][:, (h % 2) * 256:(h % 2 + 1) * 256],
                    ALU.mult, ALU.add, [("S", h), ("eb", p), ("ps", bk[h // 2])], [("S", h)])
                if seg_j >= 0:
                    cp(Sbf[:, h, :], Sst[:, h, :], [("S", h)], [("Sbf", h)], eng="act")
            if seg_j < 0:
                dma(gst_s[c], Sst[:, :, :].rearrange("p h v -> p (h v)"), [("S", h) for h in range(4)], [("gst_s", c)], q="pool")
            if full:
                for h in range(4):
                    o_ps = PS[bo[h // 2]][0:64, (h % 2) * 256:(h % 2 + 1) * 256]
                    act(e1t[:, 0:256], o_ps, AF.Square, [("ps", bo[h // 2])], ["e1t", ("ssq", p, h)], accum=ssq[:, h:h + 1])
                act(ssq[:, :], ssq[:, :], AF.Ln, [("ssq", p, h) for h in range(4)] + ["epst"], [("ssq", p, h) for h in range(4)],
                    bias=epst[0:64, 0:1], scale=1.0 / 256)
                act(ssq[:, :], ssq[:, :], AF.Exp, [("ssq", p, h) for h in range(4)], [("ssq", p, h) for h in range(4)], scale=-0.5)
                for h in range(4):
                    o_ps = PS[bo[h // 2]][0:64, (h % 2) * 256:(h % 2 + 1) * 256]
                    stt(ogt[:, h * 256:(h + 1) * 256], o_ps, ssq[:, h:h + 1], gg_tm[:, c, h * 256:(h + 1) * 256],
                        ALU.mult, ALU.mult, [("ps", bo[h // 2]), ("ssq", p, h), ("gg_tm", c)], [("og", p, h)])

        def stageB2(c):
            if not full:
                return
            p = c % 2
            ogt = ogt2[p]
            bt = P.bank()
            for k in range(8):
                tr(PSB[bt][:, k * 64:(k + 1) * 64], ogt[:, k * 128:(k + 1) * 128], identb[0:64, 0:64],
                   [("og", p, k // 2), "identb"], [("ps", bt)])
            cp(ogT[:, :, c * 64:(c + 1) * 64], PSB[bt][:, 0:512].rearrange("p (k t) -> p k t", k=8),
               [("ps", bt)], [("ogT", c)], eng="act")

        stageA(0)
        for c in range(NCH):
            if c + 1 < NCH:
                stageA(c + 1)
            stageB1(c)
            if c >= 1:
                stageB2(c - 1)
        stageB2(NCH - 1)
        if full:
            sls = []
            for f2 in range(4):
                sls.append(wload(gout_v[:, 2 * f2:2 * f2 + 2, :], [128, 2, 1024], "b_gout"))
            for m in range(8):
                b = P.bank()
                for k in range(8):
                    sv2, sk2 = sls[k // 2]
                    mm(PS[b][:, 0:GT], sv2[:, k % 2, m * 128:(m + 1) * 128], ogT[:, k, :], k == 0, k == 7,
                       [sk2] + [("ogT", c) for c in range(NCH)], [("ps", b)])
                tt(xT[xb][:, m, cols], PS[b][:, 0:GT], xT[xb][:, m, cols], ALU.add, [("ps", b), xk(xb, m)], [xk(xb, m)])

    for ti, (t0, T, j) in enumerate(tiles):
        xb = ti % 2
        load_x_tokmajor(xb, t0, T)
        ffn(xb, T, 0, 0, 0, first=(ti == 0))
        store_x(xb, XA, t0, T, ("XA", ti))
        if j >= 0:
            norm_x(xb, T, 1)
            memset(Sst[:, :, :], 0.0, [("S", h) for h in range(4)])
            memset(Dacc[:, :], 1.0, ["Dacc"])
            for bi in range(T // GT):
                gla_block(xb, bi * GT, False, j, bi, T)
            dma(gx_in[j].ap()[:, 0:1024], Sst[:, :, :].rearrange("p h v -> p (h v)"), [("S", h) for h in range(4)], [("gx_in", j)], q="pool")
            dma(gx_in[j].ap()[:, 1024:1032], Dacc[:, :], ["Dacc"], [("gx_in", j)], q="pool")
            allgather(gx_in[j], gx_out[j], ("gx_in", j), ("gx_out", j))
            emit_conv(2, n2_per_tile)

    emit_conv(2, 10 ** 6)
    memset(Rch[:, :], 0.0, ["R"] + STG)
    for g in range(2 * NSEG):
        rr, jj = g % 2, g // 2
        gb = gxb[g % 2]
        dma(gb[:, :], gx_out[jj].ap()[rr * 128:(rr + 1) * 128, :], [("gx_out", jj)] + STG, [("gxb", 0)] + STG, q="sp")
        if rr == 0:
            ts(Smt[:, :], Rch[:, :], rselt[:, 0:1], ALU.mult, ["R", "gains"] + STG, ["Sm"] + STG)
        else:
            stt(Smt[:, :], Rch[:, :], rselt[:, 1:2], Smt[:, :], ALU.mult, ALU.add, ["R", "gains", "Sm"] + STG, ["Sm"] + STG)
            dma(smine.ap()[jj], Smt[:, :], ["Sm"] + STG, [("smine", jj)] + STG, q="pool")
        for h in range(4):
            stt(Rch[:, h * 256:(h + 1) * 256], Rch[:, h * 256:(h + 1) * 256], gb[:, 1024 + h:1025 + h],
                gb[:, h * 256:(h + 1) * 256], ALU.mult, ALU.add, ["R", ("gxb", 0)] + STG, ["R"] + STG)
    dma(gst_p, Rch[:, :], ["R"] + STG, ["gst_p"] + STG, q="pool")

    dn_v = b_dn.ap().rearrange("(kc p) n -> p kc n", p=128)
    for ti, (t0, T, j) in enumerate(tiles):
        xb = ti % 2
        load_x(xb, XA, t0, T, ("XA", ti))
        norm_x(xb, T, 1)
        if j >= 0:
            dma(Sst[:, :, :].rearrange("p h v -> p (h v)"), smine.ap()[j], [("smine", j)], [("S", h) for h in range(4)], q="sp")
            cp(Sbf[:, :, :], Sst[:, :, :], [("S", h) for h in range(4)], [("Sbf", h) for h in range(4)], eng="act")
        for bi in range(T // GT):
            gla_block(xb, bi * GT, True, j, bi, T)
        ffn(xb, T, 0, 1, 2)
        ffn(xb, T, 1, 0, 3)
        store_x(xb, XB, t0, T, ("XB", ti))
        norm_x(xb, T, 4)
        dma(ropeCt[:, 0:T], ropeC[:, t0:t0 + T], [], ["ropeC"], q="sp")
        dma(ropeSt[:, 0:T], ropeS[:, t0:t0 + T], [], ["ropeS"], q="sp")
        slabs = [wload(dn_v[:, :, s_ * 256:(s_ + 1) * 256], [128, 8, 256], "b_dn") for s_ in range(3)]

        def dnproj(col0, M):
            sv, sk = slabs[col0 // 256]
            lc = col0 % 256
            b = P.bank()
            for k in range(8):
                mm(PS[b][0:M, 0:T], sv[:, k, lc:lc + M], hT[:, k, 0:T], k == 0, k == 7, [sk, ("h", k)], [("ps", b)])
            return b

        for i in range(3):
            b = dnproj(i * 128, 128)
            cp(scr[:, i, 0:T], PS[b][:, 0:T], [("ps", b)], [("scr", i)], eng=("act" if i % 2 else "dve"))
        for i in range(2):
            b = dnproj(384 + i * 128, 128)
            cp(scr[:, 3 + i, 0:T], PS[b][:, 0:T], [("ps", b)], [("scr", 3 + i)], eng=("act" if i % 2 else "dve"))
        b1 = dnproj(640, 64)
        b2 = dnproj(704, 64)
        tt(rstd[0:64, 0:T], PS[b1][0:64, 0:T], ropeCt[:, 0:T], ALU.mult, [("ps", b1), "ropeC"], ["rstd"])
        tt(scr[0:64, 5, 0:T], PS[b2][0:64, 0:T], ropeSt[:, 0:T], ALU.mult, [("ps", b2), "ropeS"], [("scr", 5)])
        tt(scr[0:64, 5, 0:T], scr[0:64, 5, 0:T], rstd[0:64, 0:T], ALU.add, [("scr", 5), "rstd"], [("scr", 5)])
        rmsnorm([(scr[:, i, 0:T], ("scr", i)) for i in range(3)], [(hT[:, i, 0:T], ("h", i)) for i in range(3)],
                [qng[:, i:i + 1] for i in range(3)], 384, T)
        dma(cqn_s.ap()[:, :, t0:t0 + T], hT[:, 0:3, 0:T], [("h", i) for i in range(3)], [("cqn_s", ti)], q="pool")
        rmsnorm([(scr[:, 3 + i, 0:T], ("scr", 3 + i)) for i in range(2)], [(scr[:, 3 + i, 0:T], ("scr", 3 + i)) for i in range(2)],
                [kvng[:, i:i + 1] for i in range(2)], 256, T)
        for i in range(2):
            cp(kbf[:, i, 0:T], scr[:, 3 + i, 0:T], [("scr", 3 + i)], [("kbf", i)], eng=("act" if i else "dve"))
        cp(kbf[0:64, 2, 0:T], scr[0:64, 5, 0:T], [("scr", 5)], [("kbf", 2)])
        if j >= 0:
            kdst = kxk_in[j].ap()[:, :]
            kkey = ("kxk_in", j)
        else:
            kdst = ksk.ap()[:, :]
            kkey = "ksk"
        dma(kdst.rearrange("(c p) t -> p c t", p=128), kbf[:, :, 0:T], [("kbf", 0), ("kbf", 1), ("kbf", 2)], [kkey], q="pool")
        nu = T // 128
        for i in range(2):
            b = P.bank()
            for u in range(nu):
                tr(PS[b][:, u * 128:(u + 1) * 128], scr[:, 3 + i, u * 128:(u + 1) * 128], ident[:, :],
                   [("scr", 3 + i), "ident"], [("ps", b)])
            cp(stv[:, 0:nu, i * 128:(i + 1) * 128], PS[b][:, 0:nu * 128].rearrange("p (u c) -> p u c", u=nu),
               [("ps", b)], ["stv"], eng=("act" if i else "dve"))
        b = P.bank()
        for u in range(nu):
            tr(PS[b][:, u * 64:(u + 1) * 64], scr[0:64, 5, u * 128:(u + 1) * 128], ident[0:64, 0:64],
               [("scr", 5), "ident"], [("ps", b)])
        cp(st64[:, 0:nu, :], PS[b][:, 0:nu * 64].rearrange("p (u c) -> p u c", u=nu), [("ps", b)], ["st64"])
        cp(stvb[:, 0:nu, :], stv[:, 0:nu, :], ["stv"], ["stvb"], eng="act")
        dma(ckv_out[t0:t0 + T, :].rearrange("(u p) c -> p u c", p=128), stv[:, 0:nu, :], ["stv"], [("ckv_out", ti)], q="pool")
        dma(kpe_out[t0:t0 + T, :].rearrange("(u p) c -> p u c", p=128), st64[:, 0:nu, :], ["st64"], [("kpe_out", ti)], q="pool")
        if j >= 0:
            dma(kxv_in[j].ap()[:, :].rearrange("(u p) c -> p u c", p=128), stvb[:, 0:nu, :], ["stvb"], [("kxv_in", j)], q="pool")
            allgather(kxk_in[j], kxk_out[j], ("kxk_in", j), ("kxk_out", j))
            allgather(kxv_in[j], kxv_out[j], ("kxv_in", j), ("kxv_out", j))
            emit_conv(3, n3_per_tile)
        else:
            dma(ksv.ap()[:, :].rearrange("(u p) c -> p u c", p=128), stvb[:, 0:nu, :], ["stvb"], ["ksv"], q="pool")

    emit_conv(3, 10 ** 6)
    barrier()

    uq_v = b_uq.ap().rearrange("(kc p) n -> p kc n", p=128)
    ukT_v = b_ukT.ap().rearrange("(h p) c -> p h c", p=128)
    uv_v = b_uv.ap().rearrange("(cc p) n -> p cc n", p=128)
    mo_v = b_mo.ap().rearrange("(f p) n -> p f n", p=128)
    unit_n = [0]
    kv_n = [0]
    memset(qpe[64:128, :, :], 0.0, [("qpe", h) for h in range(8)], eng="pool")
    for i_ in range(3):
        memset(Vs[i_][:, :, 256:264], 1.0, [("Vs", i_)])

    SBK, TBK, OBK = [0, 1, 2], [3, 4], [5, 6]
    NM0 = float(-MLA_SCALE * NEG)

    def att_unit(h, up, step, qn, qc0, KT, KTk, kn, Vlist, maskap, maskkey, pre=None, post=None):
        n = unit_n[0]
        unit_n[0] += 1
        sl, s2 = n % 3, n % 2
        bs, bt, bo = SBK[sl], TBK[s2], OBK[s2]
        st = smt[sl]
        nkb = len(Vlist)
        kpm = max(kp for (_, kp, _) in Vlist)
        if maskap is not None:
            src, srck = Ssb[s2][0:qn, 0:kn], ("Ssb", s2)
        else:
            src, srck = PS[bs][0:qn, 0:kn], ("ps", bs)
        nm_old = nmr[0:qn, h, up, (step + 1) % 2:(step + 1) % 2 + 1]
        nm_new = nmr[0:qn, h, up, step % 2:step % 2 + 1]
        ko, kn_ = ("nm", h, up, (step + 1) % 2), ("nm", h, up, step % 2)

        def qk():
            if pre is not None:
                pre()
            for c in range(2):
                mm(PS[bs][0:qn, 0:kn], qlat[:, 2 * h + c, qc0:qc0 + qn], KT(c), c == 0, False,
                   [("qlat", h), KTk], [("ps", bs)])
            mm(PS[bs][0:qn, 0:kn], qpe[:, h, qc0:qc0 + qn], KT(2), False, True, [("qpe", h), KTk], [("ps", bs)])
            if maskap is not None:
                tt(src, PS[bs][0:qn, 0:kn], maskap, ALU.add, [("ps", bs), maskkey], [srck])

        def sm():
            srcs = src.rearrange("p (a b) -> p a b", b=4)[:, :, 0]
            P.add("dve", lambda e, o=st[0:qn, 0:1], i=srcs: e.tensor_reduce(o, i, AX.X, ALU.max), [srck], [("smt", sl, 0)])
            ts(nm_new, st[0:qn, 0:1], -MLA_SCALE, ALU.mult, [("smt", sl, 0), ko], [kn_], s2=nm_old, op1=ALU.min)
            act(st[0:qn, 3:4], nm_old, AF.Exp, [ko, kn_], [("smt", sl, 3)], bias=nm_new, scale=-1.0)
            act(Pbf[sl][0:qn, 0:kn], src, AF.Exp, [srck, kn_], [("P", sl)], bias=nm_new, scale=MLA_SCALE)

        def tp():
            for kb, (Vap, kp, Vk) in enumerate(Vlist):
                tr(PSB[bt][0:kp, kb * 128:kb * 128 + qn], Pbf[sl][0:qn, kb * 128:kb * 128 + kp], identb[0:qn, 0:qn],
                   [("P", sl), "identb"], [("ps", bt)])
            cp(PTs[s2][0:kpm, 0:nkb, 0:qn], PSB[bt][0:kpm, 0:nkb * 128].rearrange("p (k q) -> p k q", k=nkb)[:, :, 0:qn],
               [("ps", bt)], [("PT", s2)], eng=("act" if n % 2 else "dve"))

        def pv():
            for kb, (Vap, kp, Vk) in enumerate(Vlist):
                mm(PS[bo][0:qn, 0:264], PTs[s2][0:kp, kb, 0:qn], Vap, kb == 0, kb == nkb - 1, [("PT", s2), Vk], [("ps", bo)])
            stt(Oacc[up][0:qn, h, :], Oacc[up][0:qn, h, :], st[0:qn, 3:4], PS[bo][0:qn, 0:264], ALU.mult, ALU.add,
                [("O", up, h), ("smt", sl, 3), ("ps", bo)], [("O", up, h)])
            if post is not None:
                post()

        return (qk, sm, tp, pv)

    def run_units(units):
        N = len(units)
        for i in range(N + 3):
            if i < N:
                units[i][0]()
            if 0 <= i - 1 < N:
                units[i - 1][1]()
            if 0 <= i - 2 < N:
                units[i - 2][2]()
            if 0 <= i - 3 < N:
                units[i - 3][3]()

    def att_init(up):
        memset(nmr[:, :, up, :], NM0, [("nm", h, up, q_) for h in range(8) for q_ in range(2)])
        memset(Oacc[up][:, :, :], 0.0, [("O", up, h) for h in range(8)], eng="pool")

    def att_fin(up, qn, qc0):
        P.add("dve", lambda e, o=linv8[0:qn, :], i=Oacc[up][0:qn, :, 256]: e.reciprocal(o, i), [("O", up, h) for h in range(8)], ["linv8"])
        for h in range(8):
            if h % 2:
                act(Obf8[0:qn, h, :], Oacc[up][0:qn, h, 0:256], AF.Copy, [("O", up, h), "linv8"], [("Obf", h)], scale=linv8[0:qn, h:h + 1])
            else:
                ts(Obf8[0:qn, h, :], Oacc[up][0:qn, h, 0:256], linv8[0:qn, h:h + 1], ALU.mult, [("O", up, h), "linv8"], [("Obf", h)])
        for hg in range(2):
            bt = 7
            for hh in range(4):
                h = hg * 4 + hh
                for c in range(2):
                    tr(PSB[bt][:, (hh * 2 + c) * 128:(hh * 2 + c) * 128 + qn], Obf8[0:qn, h, c * 128:(c + 1) * 128], identb[0:qn, 0:qn],
                       [("Obf", h), "identb"], [("ps", bt)])
            cp(abuf[:, 8 * hg:8 * hg + 8, qc0:qc0 + qn], PSB[bt][:, 0:1024].rearrange("p (c q) -> p c q", c=8)[:, :, 0:qn],
               [("ps", bt)], [("a", 8 * hg + i_) for i_ in range(8)], eng=("act" if hg else "dve"))

    for ti, (t0, T, j) in enumerate(tiles):
        xb = ti % 2
        load_x(xb, XB, t0, T, ("XB", ti))
        dma(cqn[:, :, 0:T], cqn_s.ap()[:, :, t0:t0 + T], [("cqn_s", ti)], ["cqn"], q="sp")
        dma(ropeCt[:, 0:T], ropeC[:, t0:t0 + T], [], ["ropeC"], q="sp")
        dma(ropeSt[:, 0:T], ropeS[:, t0:t0 + T], [], ["ropeS"], q="sp")
        for h in range(8):
            sv, sk = wload(uq_v[:, :, h * 256:(h + 1) * 256], [128, 3, 256], "b_uq")
            uk_v, uk_k = wload(ukT_v[:, h, :], [128, 256], "b_ukT")
            bn, b1, b2 = P.bank(), P.bank(), P.bank()
            for k in range(3):
                mm(PS[bn][:, 0:T], sv[:, k, 0:128], cqn[:, k, 0:T], k == 0, k == 2, [sk, "cqn"], [("ps", bn)])
            for k in range(3):
                mm(PS[b1][0:64, 0:T], sv[:, k, 128:192], cqn[:, k, 0:T], k == 0, k == 2, [sk, "cqn"], [("ps", b1)])
            for k in range(3):
                mm(PS[b2][0:64, 0:T], sv[:, k, 192:256], cqn[:, k, 0:T], k == 0, k == 2, [sk, "cqn"], [("ps", b2)])
            qn_ = qnope[h % 2]
            cp(qn_[:, 0:T], PS[bn][:, 0:T], [("ps", bn)], [("qnope", h % 2)], eng="act")
            tt(Ssb[0][0:64, 0:T], PS[b1][0:64, 0:T], ropeCt[:, 0:T], ALU.mult, [("ps", b1), "ropeC"], [("Ssb", 0)])
            tt(Ssb[1][0:64, 0:T], PS[b2][0:64, 0:T], ropeSt[:, 0:T], ALU.mult, [("ps", b2), "ropeS"], [("Ssb", 1)])
            tt(qpe[0:64, h, 0:T], Ssb[0][0:64, 0:T], Ssb[1][0:64, 0:T], ALU.add, [("Ssb", 0), ("Ssb", 1)], [("qpe", h)])
            for c in range(2):
                b = P.bank()
                mm(PS[b][:, 0:T], uk_v[:, c * 128:(c + 1) * 128], qn_[:, 0:T], True, True, [uk_k, ("qnope", h % 2)], [("ps", b)])
                cp(qlat[:, 2 * h + c, 0:T], PS[b][:, 0:T], [("ps", b)], [("qlat", h)], eng=("act" if c else "dve"))
        if j >= 0:
            nseg_k = 2 * j + 2
            pairs = [(u, s_) for u in range(4) for s_ in range(nseg_k)]

            def kvload(pi):
                u_, s_ = pairs[pi]
                rr, jj = s_ % 2, s_ // 2
                kvs = (kv_n[0] + pi) % 3
                dma(KTs[kvs][:, :, :], kxk_out[jj].ap()[rr * 384:(rr + 1) * 384, :].rearrange("(c p) t -> p c t", p=128),
                    [("kxk_out", jj)], [("KTs", kvs)], q="sp")
                dma(Vs[kvs][:, :, 0:256], kxv_out[jj].ap()[rr * SEG:(rr + 1) * SEG, :].rearrange("(k p) c -> p k c", p=128),
                    [("kxv_out", jj)], [("Vs", kvs)], q="sp")

            units = []
            for u in range(4):
                up = u % 2
                for s_ in range(nseg_k):
                    pi = u * nseg_k + s_
                    kvs = (kv_n[0] + pi) % 3
                    mi = s_ - 2 * j
                    mt = None
                    if mi >= 0:
                        mt = mtile[mi]
                    for h in range(8):
                        pre = None
                        post = None
                        if h == 0:
                            def pre(pi=pi, mi=mi, u=u, s_=s_, up=up):
                                if s_ == 0:
                                    att_init(up)
                                if pi == 0:
                                    kvload(0)
                                if pi + 1 < len(pairs):
                                    kvload(pi + 1)
                                if mi >= 0:
                                    dma(mtile[mi][:, :], amask[mi, u], [], [("mtile", mi)], q="sp")
                        if h == 7 and s_ == nseg_k - 1:
                            def post(u=u, up=up):
                                att_fin(up, 128, u * 128)
                        nkb_ = 4
                        kn_u = nkb_ * 128
                        units.append(att_unit(h, up, s_, 128, u * 128,
                                              lambda c, kvs=kvs, kn_u=kn_u: KTs[kvs][:, c, 0:kn_u], ("KTs", kvs), kn_u,
                                              [(Vs[kvs][:, kb, :], 128, ("Vs", kvs)) for kb in range(nkb_)],
                                              None if mt is None else mt[:, 0:kn_u], None if mt is None else ("mtile", mi),
                                              pre=pre, post=post))
            run_units(units)
            kv_n[0] += len(pairs)
        else:
            barrier()
            memset(KTp[64:128, 2, :], 0.0, ["KTp"])
            memset(KTn[64:128, 2, :], 0.0, ["KTn"])
            memset(Vp[:, :, 256:264], 1.0, ["Vp"])
            memset(Vn[:, 256:264], 1.0, ["Vn"])
            for sq_ in range(NSAMP):
                dma(Vp[:, :, 0:256], cckv[sq_].rearrange("(k p) c -> p k c", p=128), [], ["Vp"], q="pool")
                dma(kpp[:, :, :], ckpe[sq_].rearrange("(k p) c -> p k c", p=128), [], ["kpp"], q="pool")
                for c in range(2):
                    bt = P.bank()
                    for kb in range(8):
                        tr(PSB[bt][:, kb * 128:(kb + 1) * 128], Vp[:, kb, c * 128:(c + 1) * 128], identb[:, :],
                           ["Vp", "identb"], [("ps", bt)])
                    cp(KTp[:, c, :], PSB[bt][:, 0:1024], [("ps", bt)], ["KTp"], eng=("act" if c else "dve"))
                bt = P.bank()
                for kb in range(8):
                    tr(PSB[bt][0:64, kb * 128:(kb + 1) * 128], kpp[:, kb, :], identb[:, :], ["kpp", "identb"], [("ps", bt)])
                cp(KTp[0:64, 2, :], PSB[bt][0:64, 0:1024], [("ps", bt)], ["KTp"])
                dma(KTn[:, 0:2, :], ksk.ap()[0:256, sq_ * SL:(sq_ + 1) * SL].rearrange("(c p) t -> p c t", p=128), ["ksk"], ["KTn"], q="sp")
                dma(KTn[0:64, 2, :], ksk.ap()[256:320, sq_ * SL:(sq_ + 1) * SL], ["ksk"], ["KTn"], q="sp")
                dma(Vn[:, 0:256], ksv.ap()[sq_ * SL:(sq_ + 1) * SL, :], ["ksv"], ["Vn"], q="sp")
                up = sq_ % 2
                att_init(up)
                units = []
                for pb in range(2):
                    for h in range(8):
                        units.append(att_unit(h, up, pb, 64, sq_ * SL, lambda c, pb=pb: KTp[:, c, pb * 512:(pb + 1) * 512], "KTp", 512,
                                              [(Vp[:, pb * 4 + kb, :], 128, "Vp") for kb in range(4)], None, None))
                for h in range(8):
                    units.append(att_unit(h, up, 2, 64, sq_ * SL, lambda c: KTn[:, c, :], "KTn", SL, [(Vn[:, :], 64, "Vn")], None, None))
                run_units(units)
                att_fin(up, 64, sq_ * SL)
        svv, svk = wload(uv_v, [128, 2, 1024], "b_uv")
        for h in range(8):
            b = P.bank()
            for c in range(2):
                mm(PS[b][:, 0:T], svv[:, c, h * 128:(h + 1) * 128], abuf[:, 2 * h + c, 0:T], c == 0, c == 1,
                   [svk, ("a", 2 * h + c)], [("ps", b)])
            cp(hT[:, h, 0:T], PS[b][:, 0:T], [("ps", b)], [("h", h)], eng=("act" if h % 2 else "dve"))
        sls = [wload(mo_v[:, 2 * f2:2 * f2 + 2, :], [128, 2, 1024], "b_mo") for f2 in range(4)]
        for m in range(8):
            b = P.bank()
            for k in range(8):
                sv2, sk2 = sls[k // 2]
                mm(PS[b][:, 0:T], sv2[:, k % 2, m * 128:(m + 1) * 128], hT[:, k, 0:T], k == 0, k == 7,
                   [sk2, ("h", k)], [("ps", b)])
            tt(xT[xb][:, m, 0:T], PS[b][:, 0:T], xT[xb][:, m, 0:T], ALU.add, [("ps", b), xk(xb, m)], [xk(xb, m)])
        ffn(xb, T, 1, 1, 5)
        norm_x(xb, T, 6, dst_fp32=True)
        nu = T // 128
        for u in range(nu):
            for kg in range(2):
                b = P.bank()
                for kk in range(4):
                    k = kg * 4 + kk
                    tr(PS[b][:, kk * 128:(kk + 1) * 128], xT[xb][:, k, u * 128:(u + 1) * 128], ident[:, :],
                       [xk(xb, k), "ident"], [("ps", b)])
                cp(stage[:, u, kg * 512:(kg + 1) * 512], PS[b][:, :], [("ps", b)], STG, eng=("act" if kg else "dve"))
        dma(y_out[t0:t0 + T, :].rearrange("(u p) d -> p u d", p=128), stage[:, 0:nu, :], STG, [("y_out", ti)], q="pool")

    barrier()

    P.finalize()
    nsig = {e: sum(1 for o in P.q[e] if (not o.dma) and o.sig) for e in ("pe", "act", "dve", "pool")}
    sems = {e: [nc.alloc_semaphore(f"s_{e}{i}") for i in range(nsig[e] // SEMCH + 1)] for e in ("pe", "act", "dve", "pool")}
    lanes = {(q, i): nc.alloc_semaphore(f"l_{q}{i}") for q in ("sp", "pool", "act") for i in range(NLANE)}
    with nc.Block() as block:
        @block.sync
        def _(e):
            P.emit("sp", e, sems, lanes)

        @block.tensor
        def _(e):
            P.emit("pe", e, sems, lanes)

        @block.scalar
        def _(e):
            P.emit("act", e, sems, lanes)

        @block.vector
        def _(e):
            P.emit("dve", e, sems, lanes)

        @block.gpsimd
        def _(e):
            P.emit("pool", e, sems, lanes)
    es.close()
    return nc, P


_CACHE = {}


def _rope_tables(pos):
    half = 32
    inv = (np.float32(10000.0) ** (-np.arange(half, dtype=np.float32) / np.float32(half))).astype(np.float32)
    ang = (pos.astype(np.float32)[:, None] * inv[None, :]).astype(np.float32)
    cos, sin = np.cos(ang).astype(np.float32), np.sin(ang).astype(np.float32)
    C = np.concatenate([cos, cos], axis=1).T
    S = np.concatenate([-sin, sin], axis=1).T
    return np.ascontiguousarray(C), np.ascontiguousarray(S)


def kernel(x_prompt, x_sample, state_gla, cache_ckv, cache_kpe,
           norm_ffn1, w_ffn1_gate, w_ffn1_up, w_ffn1_down, norm_mix,
           norm_ffn2, w_ffn2_gate, w_ffn2_up, w_ffn2_down,
           w_gla_in, w_gla_gk_up, b_gla_gk, gla_out_norm, w_gla_out,
           w_mla_down, mla_q_norm, w_mla_uq, mla_kv_norm, w_mla_uk, w_mla_uv, w_mla_out,
           norm_final):
    f = lambda a: np.ascontiguousarray(np.asarray(a, dtype=np.float32))
    x_prompt, x_sample = f(x_prompt), f(x_sample)
    B, L, _ = x_prompt.shape
    NSEG = L // (2 * SEG)
    NTP = NSEG * SEG
    NTOK = NTP + NSAMP * SL
    if NSEG not in _CACHE:
        _CACHE[NSEG] = build(NSEG)[0]
    nc = _CACHE[NSEG]

    wdn = f(w_mla_down)[0]
    perm = np.concatenate([np.arange(32, 64), np.arange(0, 32)])
    w_dn = np.concatenate([wdn, wdn[:, 640 + perm]], axis=1)
    wuq = f(w_mla_uq)[0].reshape(384, 8, 192)
    w_uq = np.concatenate([wuq, wuq[:, :, 128 + perm]], axis=2).reshape(384, 2048)
    w_ukT = f(w_mla_uk)[0].transpose(1, 2, 0).reshape(1024, 256)
    w_uv = f(w_mla_uv)[0].reshape(256, 1024)
    gkw = np.zeros((33, 512), np.float32)
    gkw[0:16] = f(w_gla_gk_up)[0]
    gkw[32] = f(b_gla_gk)[0]
    gl = [f(norm_ffn1)[0], f(norm_mix)[0], f(norm_ffn2)[0], f(norm_ffn1)[1], f(norm_mix)[1], f(norm_ffn2)[1], f(norm_final)]
    gains = np.concatenate([g.reshape(8, 128).T for g in gl], axis=1)
    U = np.triu(np.ones((64, 64), np.float32))
    Lm = np.tril(np.ones((64, 64), np.float32), -1)
    tri = np.concatenate([U, Lm], axis=1) * np.float32(-1.0 / 16.0)
    shared = {
        "w_gin": f(w_gla_in)[0], "w_gout": f(w_gla_out)[0], "gkw": gkw, "w_dn": np.ascontiguousarray(w_dn),
        "w_uq": np.ascontiguousarray(w_uq), "w_ukT": np.ascontiguousarray(w_ukT), "w_uv": np.ascontiguousarray(w_uv),
        "w_mo": f(w_mla_out)[0], "gains": np.ascontiguousarray(gains),
        "qn_g": np.ascontiguousarray(f(mla_q_norm)[0].reshape(3, 128).T),
        "kvn_g": np.ascontiguousarray(f(mla_kv_norm)[0].reshape(2, 128).T),
        "gon": f(gla_out_norm).reshape(1, 256), "ident": np.eye(128, dtype=np.float32),
        "tri": np.ascontiguousarray(tri), "umask": U,
    }
    for l in range(2):
        shared[f"wg{l}0"], shared[f"wu{l}0"], shared[f"wd{l}0"] = f(w_ffn1_gate)[l], f(w_ffn1_up)[l], f(w_ffn1_down)[l]
        shared[f"wg{l}1"], shared[f"wu{l}1"], shared[f"wd{l}1"] = f(w_ffn2_gate)[l], f(w_ffn2_up)[l], f(w_ffn2_down)[l]
    state_gla, cache_ckv, cache_kpe = f(state_gla), f(cache_ckv), f(cache_kpe)
    in_maps = []
    for c in range(8):
        p, r = c // 2, c % 2
        segs = [x_prompt[p, (2 * j + r) * SEG:(2 * j + r + 1) * SEG] for j in range(NSEG)]
        xs = x_sample[NSAMP * c:NSAMP * (c + 1)].reshape(NSAMP * SL, D)
        x_in = np.concatenate(segs + [xs], axis=0)
        pos = np.concatenate([np.arange((2 * j + r) * SEG, (2 * j + r + 1) * SEG) for j in range(NSEG)]
                             + [PAST + np.arange(SL)] * NSAMP)
        C, S = _rope_tables(pos)
        am = np.zeros((2, 4, 128, 512), np.float32)
        qchunk = (np.arange(512) // 64).reshape(4, 128)
        kchunk = np.arange(512) // 64
        diag = np.where(kchunk[None, None, :] <= qchunk[:, :, None], 0.0, NEG).astype(np.float32)
        if r == 0:
            am[0] = diag
            am[1] = NEG
        else:
            am[0] = 0.0
            am[1] = diag
        m = dict(shared)
        m.update({
            "x_in": np.ascontiguousarray(x_in), "st_in": np.ascontiguousarray(state_gla[0, NSAMP * c:NSAMP * (c + 1)]),
            "cckv": np.ascontiguousarray(cache_ckv[0, NSAMP * c:NSAMP * (c + 1)]),
            "ckpe": np.ascontiguousarray(cache_kpe[0, NSAMP * c:NSAMP * (c + 1)]),
            "ropeC": C, "ropeS": S, "amask": am,
            "rsel": np.ascontiguousarray(np.tile(np.array([[1.0 - r, float(r)]], np.float32), (128, 1))),
        })
        in_maps.append(m)
    res = run_bass_kernel_spmd(nc, in_maps, core_ids=list(range(8)))
    R = res.results
    y_p = np.zeros((B, L, D), np.float32)
    ckv_p = np.zeros((1, B, L, 256), np.float32)
    kpe_p = np.zeros((1, B, L, 64), np.float32)
    y_s = np.zeros((8 * NSAMP, SL, D), np.float32)
    ckv_s = np.zeros((1, 8 * NSAMP, SL, 256), np.float32)
    kpe_s = np.zeros((1, 8 * NSAMP, SL, 64), np.float32)
    gla_p = np.zeros((1, B, 4, 128, 256), np.float32)
    gla_s = np.zeros((1, 8 * NSAMP, 4, 128, 256), np.float32)
    for c in range(8):
        p, r = c // 2, c % 2
        o = R[c]
        for j in range(NSEG):
            g0 = (2 * j + r) * SEG
            y_p[p, g0:g0 + SEG] = o["y_out"][j * SEG:(j + 1) * SEG]
            ckv_p[0, p, g0:g0 + SEG] = o["ckv_out"][j * SEG:(j + 1) * SEG]
            kpe_p[0, p, g0:g0 + SEG] = o["kpe_out"][j * SEG:(j + 1) * SEG]
        y_s[NSAMP * c:NSAMP * (c + 1)] = o["y_out"][NTP:].reshape(NSAMP, SL, D)
        ckv_s[0, NSAMP * c:NSAMP * (c + 1)] = o["ckv_out"][NTP:].reshape(NSAMP, SL, 256)
        kpe_s[0, NSAMP * c:NSAMP * (c + 1)] = o["kpe_out"][NTP:].reshape(NSAMP, SL, 64)
        if r == 0:
            gla_p[0, p] = o["gst_p"].reshape(128, 4, 256).transpose(1, 0, 2)
        gla_s[0, NSAMP * c:NSAMP * (c + 1)] = o["gst_s"].reshape(NSAMP, 128, 4, 256).transpose(0, 2, 1, 3)
    return (y_p, y_s, gla_p, ckv_p, kpe_p, gla_s, ckv_s, kpe_s)
```
